# Optimizing a Trainium2 kernel written in Bass

```python
import jax, jax.numpy as jnp
from jax import lax
import numpy as np

D_MODEL = 1024
BATCH = 16
SEQ = 2048
DEPTH = 2

CTX_LEN = 256
GRID_W = 64
MIX_WIDTH = D_MODEL
GROUP_W = MIX_WIDTH // 4
HEAD_DIM = 64
CONV_CH = GROUP_W
CONV_K = 31
FNET_GROUPS = GROUP_W // HEAD_DIM
NAT_HEADS = GROUP_W // HEAD_DIM
NAT_MAX_ROWS = 8
NAT_COLS = 16
NAT_QC = 16
NAT_KC = NAT_QC + NAT_COLS
SWA_Q_HEADS = GROUP_W // HEAD_DIM
SWA_KV_HEADS = 2
SWA_WINDOW = 128
SWA_BLOCK = 128
ROPE_BASE = 10000.0
FFN_DIM = 2816
MACARON_W = 0.5
N_MOD = 9
EPS = 1e-6
NEG = -1e30
IN_WIDTHS = (2 * CONV_CH, GROUP_W,
             NAT_HEADS * HEAD_DIM, NAT_HEADS * HEAD_DIM, NAT_HEADS * HEAD_DIM,
             SWA_Q_HEADS * HEAD_DIM, SWA_KV_HEADS * HEAD_DIM, SWA_KV_HEADS * HEAD_DIM)
IN_DIM = sum(IN_WIDTHS)
IN_SPLITS = tuple(int(v) for v in np.cumsum(IN_WIDTHS)[:-1])

kernel_name = "hybrid_parallel_group_dit_block"


def rms_norm(x, g):
    x32 = x.astype(jnp.float32)
    y = x32 * lax.rsqrt(jnp.mean(x32 * x32, axis=-1, keepdims=True) + EPS)
    return (y * g.astype(jnp.float32)).astype(x.dtype)


def modulate(x, shift, scale):
    return x * (1 + scale) + shift


def ffn_sublayer(x, shift, scale, gate, g_pre, g_post, w1, w3, w2):
    h = modulate(rms_norm(x, g_pre), shift, scale)
    y = (jax.nn.silu(h @ w1) * (h @ w3)) @ w2
    return x + MACARON_W * gate * rms_norm(y, g_post)


def split_heads(t, nh):
    return t.reshape(t.shape[0], t.shape[1], nh, t.shape[-1] // nh)


def conformer_conv(u, w, b, ln_g, ln_b):
    a = u[..., :CONV_CH] * jax.nn.sigmoid(u[..., CONV_CH:])
    y = lax.conv_general_dilated(a, w[:, None, :], window_strides=(1,),
                                 padding=[(CONV_K // 2, CONV_K // 2)],
                                 dimension_numbers=('NWC', 'WIO', 'NWC'),
                                 feature_group_count=CONV_CH) + b
    y32 = y.astype(jnp.float32)
    mu = jnp.mean(y32, axis=-1, keepdims=True)
    var = jnp.mean(jnp.square(y32 - mu), axis=-1, keepdims=True)
    yn = (y32 - mu) * lax.rsqrt(var + EPS) * ln_g.astype(jnp.float32) + ln_b.astype(jnp.float32)
    return jax.nn.silu(yn).astype(u.dtype)


def fourier_mix(u):
    b, n, _ = u.shape
    z = u.astype(jnp.float32).reshape(b, n, FNET_GROUPS, -1)
    y = jnp.fft.fft2(z, axes=(1, 3), norm='ortho').real
    return y.reshape(b, n, -1).astype(u.dtype)


def rope_1d(x, pos):
    half = x.shape[-1] // 2
    inv = ROPE_BASE ** (-jnp.arange(half, dtype=jnp.float32) / half)
    ang = pos[:, None] * inv[None, :]
    cos = jnp.cos(ang)[:, None, :]
    sin = jnp.sin(ang)[:, None, :]
    x1, x2 = x[..., :half], x[..., half:]
    return jnp.concatenate([x1 * cos - x2 * sin, x1 * sin + x2 * cos], axis=-1)


def axial_rope(x, rows, cols):
    x32 = x.astype(jnp.float32)
    h = x.shape[-1] // 2
    return jnp.concatenate([rope_1d(x32[..., :h], rows), rope_1d(x32[..., h:], cols)], axis=-1).astype(x.dtype)


def nat_geometry(rows):
    wr = min(NAT_MAX_ROWS, rows)
    ncb = GRID_W // NAT_QC
    r = np.arange(rows)
    rs = np.clip(r - wr // 2, 0, rows - wr)
    kr = rs[:, None] + np.arange(wr)[None, :]
    cb = np.clip(np.arange(ncb) * NAT_QC - NAT_COLS // 2, 0, GRID_W - NAT_KC)
    kc = cb[:, None] + np.arange(NAT_KC)[None, :]
    qc = np.arange(ncb)[:, None] * NAT_QC + np.arange(NAT_QC)[None, :]
    ws = np.clip(qc - NAT_COLS // 2, 0, GRID_W - NAT_COLS)
    col_ok = (kc[:, None, :] >= ws[..., None]) & (kc[:, None, :] < ws[..., None] + NAT_COLS)
    idx = kr[:, None, :, None] * GRID_W + kc[None, :, None, :]
    dr = kr - r[:, None] + NAT_MAX_ROWS - 1
    dc = np.clip(kc[:, None, :] - qc[..., None] + NAT_COLS - 1, 0, 2 * NAT_COLS - 2)
    return wr, idx, col_ok, dr, dc


def nat_attention(q, k, v, kx, vx, rel_bias):
    b, n, h, d = q.shape
    rows = n // GRID_W
    ncb = GRID_W // NAT_QC
    wr, idx, col_ok, dr, dc = nat_geometry(rows)
    nj = wr * NAT_KC
    qb = q.reshape(b, rows, ncb, NAT_QC, h, d)
    flat = jnp.asarray(idx.reshape(-1))
    kb = jnp.take(k, flat, axis=1).reshape(b, rows, ncb, nj, h, d)
    vb = jnp.take(v, flat, axis=1).reshape(b, rows, ncb, nj, h, d)
    bias = rel_bias.astype(jnp.float32)[:, dr[:, None, None, :, None], dc[None, :, :, None, :]]
    bias = jnp.where(col_ok[None, None, :, :, None, :], bias, NEG)
    bias = bias.reshape(h, rows, ncb, NAT_QC, nj).transpose(1, 2, 0, 3, 4)
    scale = d ** -0.5
    s_loc = jnp.einsum('brcqhd,brcjhd->brchqj', qb, kb).astype(jnp.float32) * scale + bias
    s_ctx = jnp.einsum('brcqhd,blhd->brchql', qb, kx).astype(jnp.float32) * scale
    p = jax.nn.softmax(jnp.concatenate([s_loc, s_ctx], axis=-1), axis=-1).astype(v.dtype)
    o = (jnp.einsum('brchqj,brcjhd->brcqhd', p[..., :nj], vb)
         + jnp.einsum('brchql,blhd->brcqhd', p[..., nj:], vx))
    return o.reshape(b, n, h * d)


def band_attention(q, k, v, kx, vx, sink):
    b, n, hq, d = q.shape
    hkv = k.shape[2]
    g = hq // hkv
    nb = n // SWA_BLOCK
    nl = kx.shape[1]
    qb = q.reshape(b, nb, SWA_BLOCK, hkv, g, d)

    def band(t):
        tp = jnp.pad(t, ((0, 0), (SWA_BLOCK, SWA_BLOCK), (0, 0), (0, 0))).reshape(b, nb + 2, SWA_BLOCK, hkv, d)
        return jnp.concatenate([tp[:, :-2], tp[:, 1:-1], tp[:, 2:]], axis=2)

    kb, vb = band(k), band(v)
    nj = 3 * SWA_BLOCK
    rel = np.arange(nj)[None, :] - SWA_BLOCK - np.arange(SWA_BLOCK)[:, None]
    kpos = (np.arange(nb)[:, None] - 1) * SWA_BLOCK + np.arange(nj)[None, :]
    ok = (np.abs(rel) <= SWA_WINDOW)[None] & ((kpos >= 0) & (kpos < n))[:, None, :]
    scale = d ** -0.5
    s_loc = jnp.einsum('bnqkgd,bnjkd->bnkgqj', qb, kb).astype(jnp.float32) * scale
    s_loc = jnp.where(ok[None, :, None, None], s_loc, NEG)
    s_ctx = jnp.einsum('bnqkgd,blkd->bnkgql', qb, kx).astype(jnp.float32) * scale
    s_sink = jnp.broadcast_to(sink.astype(jnp.float32).reshape(hkv, g)[None, None, :, :, None, None],
                              s_loc.shape[:-1] + (1,))
    p = jax.nn.softmax(jnp.concatenate([s_loc, s_ctx, s_sink], axis=-1), axis=-1).astype(v.dtype)
    o = (jnp.einsum('bnkgqj,bnjkd->bnqkgd', p[..., :nj], vb)
         + jnp.einsum('bnkgql,blkd->bnqkgd', p[..., nj:nj + nl], vx))
    return o.reshape(b, n, hq * d)


def dense_ctx_attention(q, k, v, sink):
    bsz, nl, nk, g, d = q.shape
    s = jnp.einsum('blkgd,bmkd->bkglm', q, k).astype(jnp.float32) * d ** -0.5
    if sink is None:
        p = jax.nn.softmax(s, axis=-1)
    else:
        s_sink = jnp.broadcast_to(sink.astype(jnp.float32).reshape(nk, g)[None, :, :, None, None], s.shape[:-1] + (1,))
        p = jax.nn.softmax(jnp.concatenate([s, s_sink], axis=-1), axis=-1)[..., :-1]
    o = jnp.einsum('bkglm,bmkd->blkgd', p.astype(v.dtype), v)
    return o.reshape(bsz, nl, nk * g * d)


def token_mix(h, hc, w_in, conv_w, conv_b, conv_ln_g, conv_ln_b, nat_bias, sinks, w_out, ctx_out):
    n = h.shape[1]
    t = jnp.arange(n, dtype=jnp.int32)
    rows = (t // GRID_W).astype(jnp.float32)
    cols = (t % GRID_W).astype(jnp.float32)
    ua, ub, qn, kn, vn, qs, ks, vs = jnp.split(h @ w_in, IN_SPLITS, axis=-1)
    xa, xb, cqn, ckn, cvn, cqs, cks, cvs = jnp.split(hc @ w_in, IN_SPLITS, axis=-1)
    ckn, cvn = split_heads(ckn, NAT_HEADS), split_heads(cvn, NAT_HEADS)
    cks, cvs = split_heads(cks, SWA_KV_HEADS), split_heads(cvs, SWA_KV_HEADS)
    y_a = conformer_conv(ua, conv_w, conv_b, conv_ln_g, conv_ln_b)
    y_b = fourier_mix(ub)
    y_c = nat_attention(split_heads(qn, NAT_HEADS), split_heads(kn, NAT_HEADS), split_heads(vn, NAT_HEADS),
                        ckn, cvn, nat_bias)
    y_d = band_attention(axial_rope(split_heads(qs, SWA_Q_HEADS), rows, cols),
                         axial_rope(split_heads(ks, SWA_KV_HEADS), rows, cols),
                         split_heads(vs, SWA_KV_HEADS), cks, cvs, sinks)
    y = jnp.concatenate([y_a, y_b, y_c, y_d], axis=-1) @ w_out
    if not ctx_out:
        return y, None
    bsz, nl = hc.shape[0], hc.shape[1]
    yc_a = conformer_conv(xa, conv_w, conv_b, conv_ln_g, conv_ln_b)
    yc_b = fourier_mix(xb)
    yc_c = dense_ctx_attention(cqn.reshape(bsz, nl, NAT_HEADS, 1, HEAD_DIM), ckn, cvn, None)
    yc_d = dense_ctx_attention(cqs.reshape(bsz, nl, SWA_KV_HEADS, SWA_Q_HEADS // SWA_KV_HEADS, HEAD_DIM),
                               cks, cvs, sinks)
    yc = jnp.concatenate([yc_a, yc_b, yc_c, yc_d], axis=-1) @ w_out
    return y, yc


def setup_inputs(seed: int = 0) -> dict:
    key = jax.random.key(seed)
    ks = jax.random.split(key, 20)
    nrm = jax.random.normal
    f32 = jnp.float32
    d = D_MODEL
    return {
        'x': nrm(ks[0], (BATCH, SEQ, d), f32),
        'c': nrm(ks[1], (BATCH, d), f32),
        'ctx': nrm(ks[2], (BATCH, CTX_LEN, d), f32),
        'c_ctx': nrm(ks[3], (d,), f32),
        'w_ada': nrm(ks[4], (DEPTH, d, N_MOD * d), f32) * (0.5 * d ** -0.5),
        'b_ada': nrm(ks[5], (DEPTH, N_MOD * d), f32) * 0.01,
        'norm_g': 1.0 + 0.05 * nrm(ks[6], (DEPTH, 6, d), f32),
        'ffn_w1': nrm(ks[7], (DEPTH, 2, d, FFN_DIM), f32) * d ** -0.5,
        'ffn_w3': nrm(ks[8], (DEPTH, 2, d, FFN_DIM), f32) * d ** -0.5,
        'ffn_w2': nrm(ks[9], (DEPTH, 2, FFN_DIM, d), f32) * FFN_DIM ** -0.5,
        'w_in': nrm(ks[10], (DEPTH, d, IN_DIM), f32) * d ** -0.5,
        'conv_w': nrm(ks[11], (DEPTH, CONV_K, CONV_CH), f32) * CONV_K ** -0.5,
        'conv_b': nrm(ks[12], (DEPTH, CONV_CH), f32) * 0.02,
        'conv_ln_g': 1.0 + 0.05 * nrm(ks[13], (DEPTH, CONV_CH), f32),
        'conv_ln_b': nrm(ks[14], (DEPTH, CONV_CH), f32) * 0.02,
        'nat_rel_bias': nrm(ks[15], (DEPTH, NAT_HEADS, 2 * NAT_MAX_ROWS - 1, 2 * NAT_COLS - 1), f32) * 0.2,
        'sink_logits': nrm(ks[16], (DEPTH, SWA_Q_HEADS), f32) * 0.5,
        'w_out': nrm(ks[17], (DEPTH, MIX_WIDTH, d), f32) * MIX_WIDTH ** -0.5,
    }


def reference(x, c, ctx, c_ctx, w_ada, b_ada, norm_g, ffn_w1, ffn_w3, ffn_w2, w_in, conv_w, conv_b,
              conv_ln_g, conv_ln_b, nat_rel_bias, sink_logits, w_out):
    bsz, d = x.shape[0], x.shape[-1]
    xl, xc = x, ctx
    sc = jax.nn.silu(c)
    scc = jax.nn.silu(c_ctx)
    for l in range(DEPTH):
        last = l == DEPTH - 1
        mod_l = (sc @ w_ada[l] + b_ada[l]).reshape(bsz, 1, N_MOD, d)
        mod_c = (scc @ w_ada[l] + b_ada[l]).reshape(N_MOD, d)
        xl = ffn_sublayer(xl, mod_l[..., 0, :], mod_l[..., 1, :], mod_l[..., 2, :], norm_g[l, 0], norm_g[l, 1],
                          ffn_w1[l, 0], ffn_w3[l, 0], ffn_w2[l, 0])
        xc = ffn_sublayer(xc, mod_c[0], mod_c[1], mod_c[2], norm_g[l, 0], norm_g[l, 1],
                          ffn_w1[l, 0], ffn_w3[l, 0], ffn_w2[l, 0])
        hl = modulate(rms_norm(xl, norm_g[l, 2]), mod_l[..., 3, :], mod_l[..., 4, :])
        hc = modulate(rms_norm(xc, norm_g[l, 2]), mod_c[3], mod_c[4])
        yl, yc = token_mix(hl, hc, w_in[l], conv_w[l], conv_b[l], conv_ln_g[l], conv_ln_b[l],
                           nat_rel_bias[l], sink_logits[l], w_out[l], not last)
        xl = xl + mod_l[..., 5, :] * rms_norm(yl, norm_g[l, 3])
        xl = ffn_sublayer(xl, mod_l[..., 6, :], mod_l[..., 7, :], mod_l[..., 8, :], norm_g[l, 4], norm_g[l, 5],
                          ffn_w1[l, 1], ffn_w3[l, 1], ffn_w2[l, 1])
        if not last:
            xc = xc + mod_c[5] * rms_norm(yc, norm_g[l, 3])
            xc = ffn_sublayer(xc, mod_c[6], mod_c[7], mod_c[8], norm_g[l, 4], norm_g[l, 5],
                              ffn_w1[l, 1], ffn_w3[l, 1], ffn_w2[l, 1])
    return xl
```

```python
import numpy as np
import ml_dtypes
from contextlib import ExitStack
import concourse.bass as bass
import concourse.mybir as mybir
from concourse.bass_utils import run_bass_kernel_spmd

F32 = mybir.dt.float32
BF16 = mybir.dt.bfloat16
ALU = mybir.AluOpType
AF = mybir.ActivationFunctionType
ESZ = {F32: 4, BF16: 2}

D = 1024; NL = 2048; NC_ = 256; T = NL + NC_; DEPTH = 2; FF = 2816; NFC = FF // 128
GRID_W = 64; ROWS = NL // GRID_W
TB = 384
EPS = 1e-6
SB_BASE = 16640
SB_END = 229376
NEG = -1e30


class View:
    __slots__ = ("ap", "rects")

    def __init__(self, ap, rects):
        self.ap = ap
        self.rects = rects


class Buf:
    def __init__(self, h, space, shape, dtype, base):
        self.h = h; self.space = space; self.shape = list(shape); self.es = ESZ[dtype]; self.base = base
        st = [1] * len(shape)
        for i in range(len(shape) - 2, 0, -1):
            st[i] = st[i + 1] * shape[i + 1]
        self.st = st

    def __getitem__(self, idx):
        if not isinstance(idx, tuple):
            idx = (idx,)
        idx = list(idx) + [slice(None)] * (len(self.shape) - len(idx))
        rng = []
        for d, s in enumerate(idx):
            if isinstance(s, int):
                rng.append((s, s + 1))
            else:
                a = 0 if s.start is None else s.start
                b = self.shape[d] if s.stop is None else s.stop
                assert s.step in (None, 1) and 0 <= a < b <= self.shape[d], (idx, self.shape)
                rng.append((a, b))
        plo, phi = rng[0]
        nd = len(self.shape)
        ivs = []

        def rec(d, off):
            a, b = rng[d]
            if all(rng[e] == (0, self.shape[e]) for e in range(d + 1, nd)):
                ivs.append((off + a * self.st[d], off + b * self.st[d]))
                return
            for i in range(a, b):
                rec(d + 1, off + i * self.st[d])
        rec(1, 0)
        m = []
        for lo, hi in ivs:
            if m and m[-1][1] == lo:
                m[-1][1] = hi
            else:
                m.append([lo, hi])
        rects = [(self.space, plo, phi, self.base + lo * self.es, self.base + hi * self.es) for lo, hi in m]
        return View(self.h[tuple(idx)], rects)

    def whole(self, ap):
        n = 1
        for s in self.shape[1:]:
            n *= s
        return View(ap, [(self.space, 0, self.shape[0], self.base, self.base + n * self.es)])


class Op:
    __slots__ = ("eng", "fn", "deps", "dma", "sig", "seq", "dsem", "dval", "idx")


COMPUTE = ("pe", "act", "dve", "pool")
EPOCH = 30000


class Prog:
    def __init__(self, nc, stack):
        self.nc = nc; self.stack = stack
        self.ops = []
        self.recs = {"sb": [], "ps": []}
        self.sb_off = SB_BASE
        self.nbuf = 0
        self.psb = [Buf(stack.enter_context(nc.psum_tensor("psb%d" % i, [128, 512], F32)), "ps", [128, 512], F32, i * 2048)
                    for i in range(8)]
        self.psi = 0
        self.last = {}
        self.ndma = 0
        self.ndmaq = {}
        self.reserved = set()

    def sb(self, shape, dtype, at=None, name=None):
        n = ESZ[dtype]
        for s in shape[1:]:
            n *= s
        n = (n + 31) // 32 * 32
        if at is None:
            at = self.sb_off
            self.sb_off += n
            assert self.sb_off <= SB_END, "SBUF overflow %d" % self.sb_off
        self.nbuf += 1
        h = self.nc.alloc_sbuf_tensor_at(name or ("b%d" % self.nbuf), list(shape), dtype, offset=at)
        return Buf(h, "sb", shape, dtype, at)

    def mark(self):
        return self.sb_off

    def release(self, m):
        self.sb_off = m

    def ps(self):
        while self.psi in self.reserved:
            self.psi = (self.psi + 1) % 8
        b = self.psb[self.psi]
        self.psi = (self.psi + 1) % 8
        return b

    def ps_reserve(self, n):
        out = []
        for _ in range(n):
            while self.psi in self.reserved:
                self.psi = (self.psi + 1) % 8
            self.reserved.add(self.psi)
            out.append(self.psb[self.psi])
            self.psi = (self.psi + 1) % 8
        return out

    def ps_release(self, banks):
        for b in banks:
            self.reserved.discard(b.base // 2048)

    def _track(self, oi, eng, dma, reads, writes):
        deps = set()
        for v in reads:
            for (sp, plo, phi, lo, hi) in v.rects:
                for r in self.recs[sp]:
                    if r[2] < hi and lo < r[3] and r[0] < phi and plo < r[1]:
                        if r[4] is not None:
                            deps.add(r[4])
                        if dma:
                            r[6].append(oi)
                        else:
                            r[5][eng] = oi
        for v in writes:
            for (sp, plo, phi, lo, hi) in v.rects:
                new = []
                for r in self.recs[sp]:
                    if r[2] < hi and lo < r[3] and r[0] < phi and plo < r[1]:
                        if r[4] is not None:
                            deps.add(r[4])
                        deps.update(r[5].values())
                        deps.update(r[6])
                        if plo <= r[0] and r[1] <= phi:
                            if r[2] < lo:
                                new.append([r[0], r[1], r[2], lo, r[4], dict(r[5]), list(r[6])])
                            if hi < r[3]:
                                new.append([r[0], r[1], hi, r[3], r[4], dict(r[5]), list(r[6])])
                        else:
                            new.append(r)
                    else:
                        new.append(r)
                new.append([plo, phi, lo, hi, oi, {}, []])
                self.recs[sp] = new
        deps.discard(oi)
        return deps

    def add(self, eng, fn, reads, writes, dma=False):
        o = Op()
        o.eng = eng; o.fn = fn; o.dma = dma; o.sig = False; o.seq = None
        oi = len(self.ops)
        o.idx = oi
        rd = [v for v in reads if isinstance(v, View)]
        wr = [v for v in writes if isinstance(v, View)]
        def bankify(v):
            bs = sorted({lo // 2048 for (sp, plo, phi, lo, hi) in v.rects})
            return View(v.ap, [("ps", 0, 128, b * 2048, (b + 1) * 2048) for b in bs])
        pr = [bankify(v) for v in rd if v.rects and v.rects[0][0] == "ps"]
        rd = [v for v in rd if not (v.rects and v.rects[0][0] == "ps")]
        wr = [bankify(v) if (v.rects and v.rects[0][0] == "ps") else v for v in wr] + pr
        o.deps = self._track(oi, eng, dma, rd, wr)
        if dma:
            o.dsem = self.ndmaq.get(eng, 0)
            self.ndmaq[eng] = o.dsem + 1
            self.ndma += 1
        self.ops.append(o)
        return o

    def mm(self, out, lhsT, rhs, start=True, stop=True):
        self.add("pe", lambda e: e.matmul(out.ap, lhsT.ap, rhs.ap, start=start, stop=stop, skip_group_check=True),
                 [lhsT, rhs], [out])

    def act(self, out, in_, func, bias=0.0, scale=1.0, eng="act"):
        rd = [in_]
        b = bias.ap if isinstance(bias, View) else bias
        s = scale.ap if isinstance(scale, View) else scale
        if isinstance(bias, View):
            rd.append(bias)
        if isinstance(scale, View):
            rd.append(scale)
        self.add("act", lambda e: e.activation(out=out.ap, in_=in_.ap, func=func, bias=b, scale=s), rd, [out])

    def tt(self, eng, out, in0, in1, op):
        self.add(eng, lambda e: e.tensor_tensor(out.ap, in0.ap, in1.ap, op), [in0, in1], [out])

    def ts(self, eng, out, in0, s1, s2, op0, op1=None):
        rd = [in0] + [s for s in (s1, s2) if isinstance(s, View)]
        a1 = s1.ap if isinstance(s1, View) else s1
        a2 = s2.ap if isinstance(s2, View) else s2
        if op1 is None:
            self.add(eng, lambda e: e.tensor_scalar(out.ap, in0.ap, a1, None, op0), rd, [out])
        else:
            self.add(eng, lambda e: e.tensor_scalar(out.ap, in0.ap, a1, a2, op0, op1), rd, [out])

    def stt(self, eng, out, in0, scalar, in1, op0, op1):
        rd = [in0, in1] + ([scalar] if isinstance(scalar, View) else [])
        sc = scalar.ap if isinstance(scalar, View) else scalar
        self.add(eng, lambda e: e.scalar_tensor_tensor(out=out.ap, in0=in0.ap, scalar=sc, in1=in1.ap, op0=op0, op1=op1),
                 rd, [out])

    def copy(self, eng, out, in_):
        if eng == "act":
            self.add("act", lambda e: e.copy(out.ap, in_.ap), [in_], [out])
        else:
            self.add(eng, lambda e: e.tensor_copy(out.ap, in_.ap), [in_], [out])

    def recip(self, out, in_):
        self.add("dve", lambda e: e.reciprocal(out.ap, in_.ap), [in_], [out])

    def memset(self, eng, out, val):
        self.add(eng, lambda e: e.memset(out.ap, val), [], [out])

    def dma(self, q, out, in_):
        oa = out.ap if isinstance(out, View) else out
        ia = in_.ap if isinstance(in_, View) else in_
        self.add(q, lambda e: e.dma_start(out=oa, in_=ia), [in_], [out], dma=True)

    def emit(self, out_dma_wait_engine="sp"):
        nc = self.nc
        ops = self.ops
        KD = 24
        for o in ops:
            for d in o.deps:
                ops[d].sig = True
        cnt = {e: 0 for e in COMPUTE}
        for o in ops:
            if not o.dma and o.sig:
                o.seq = cnt[o.eng]
                cnt[o.eng] += 1
        esem = {e: [self.stack.enter_context(nc.semaphore("s_%s%d" % (e, k))) for k in range(cnt[e] // EPOCH + 1)]
                for e in COMPUTE}
        KDQ = {"sp": 12, "pool": 20, "act": 4}
        dsems = {q: [self.stack.enter_context(nc.semaphore("s_dma_%s%d" % (q, k))) for k in range(KDQ[q])]
                 for q in self.ndmaq}
        streams = {}
        for o in ops:
            streams.setdefault(o.eng, []).append(o)
        ndmaq = self.ndmaq

        def run(eng, e):
            wm = {}

            def wait(sem, val, key):
                if wm.get(key, 0) >= val:
                    return
                wm[key] = val
                e.wait_ge(sem, val)
            for o in streams.get(eng, []):
                best = {}
                for d in o.deps:
                    s = ops[d]
                    if s.dma:
                        kd = KDQ[s.eng]
                        k = s.dsem % kd
                        wait(dsems[s.eng][k], 16 * (s.dsem // kd + 1), ("d", s.eng, k))
                    else:
                        if s.eng == eng and eng == "pe":
                            continue
                        key = (s.eng, s.seq // EPOCH)
                        v = s.seq % EPOCH + 1
                        if best.get(key, 0) < v:
                            best[key] = v
                for (se, ep), v in best.items():
                    wait(esem[se][ep], v, (se, ep))
                if o.dma:
                    kd = KDQ[eng]
                    k = o.dsem % kd
                    if o.dsem >= kd:
                        wait(dsems[eng][k], 16 * (o.dsem // kd), ("d", eng, k))
                    o.fn(e).then_inc(dsems[eng][k], 16)
                else:
                    ins = o.fn(e)
                    if o.sig:
                        ins.then_inc(esem[o.eng][o.seq // EPOCH], 1)
            if eng == out_dma_wait_engine:
                for q, nq in ndmaq.items():
                    kd = KDQ[q]
                    for k in range(min(kd, nq)):
                        n = (nq - 1 - k) // kd + 1
                        wait(dsems[q][k], 16 * n, ("d", q, k))

        with nc.Block() as block:
            @block.tensor
            def _(e):
                run("pe", e)

            @block.scalar
            def _(e):
                run("act", e)

            @block.vector
            def _(e):
                run("dve", e)

            @block.gpsimd
            def _(e):
                run("pool", e)

            @block.sync
            def _(e):
                run("sp", e)


def _bf(a):
    return np.ascontiguousarray(a).astype(ml_dtypes.bfloat16)


def _consts():
    c = {}
    k = np.arange(64)
    ang = 2 * np.pi * np.outer(k, k) / 64
    C64, S64 = np.cos(ang), np.sin(ang)
    bd = np.zeros((128, 256))
    for g in range(2):
        bd[g * 64:(g + 1) * 64, g * 64:(g + 1) * 64] = C64
        bd[g * 64:(g + 1) * 64, 128 + g * 64:128 + (g + 1) * 64] = -S64
    c["bd"] = _bf(bd)
    c["ident"] = _bf(np.eye(128))
    n = np.arange(NL)
    sc = 1.0 / np.sqrt(NL * 64.0)
    angN = 2 * np.pi * ((np.outer(n, n)) % NL) / NL
    CN = (np.cos(angN) * sc).reshape(16, 128, 16, 128)
    SN = (np.sin(angN) * sc).reshape(16, 128, 16, 128)
    dl = np.stack([CN, SN], axis=3)
    c["dftl"] = _bf(dl.transpose(2, 1, 0, 3, 4))
    m = np.arange(NC_)
    scc = 1.0 / np.sqrt(NC_ * 64.0)
    angC = 2 * np.pi * ((np.outer(m, m)) % NC_) / NC_
    CC = (np.cos(angC) * scc).reshape(2, 128, NC_)
    SC = (np.sin(angC) * scc).reshape(2, 128, NC_)
    c["dftc"] = _bf(np.stack([CC, SC], axis=2).transpose(1, 0, 2, 3))
    t = np.arange(NL)
    rows = (t // GRID_W).astype(np.float64); cols = (t % GRID_W).astype(np.float64)
    inv = 10000.0 ** (-np.arange(16) / 16.0)
    cos = np.zeros((64, NL)); sin = np.zeros((64, NL))
    for d in range(64):
        pos = rows if d < 32 else cols
        i = d % 16
        a = pos * inv[i]
        cos[d] = np.cos(a)
        sin[d] = -np.sin(a) if (d % 32) < 16 else np.sin(a)
    c["rope"] = np.ascontiguousarray(np.stack([np.tile(cos, (2, 1)), np.tile(sin, (2, 1))], axis=1)).astype(np.float32)
    qc = np.arange(64)
    ws = np.clip(qc - 8, 0, 48)
    kc = np.arange(64)
    ok = (kc[:, None] >= ws[None, :]) & (kc[:, None] < ws[None, :] + 16)
    mk = np.where(ok, 0.0, NEG)
    c["natmask"] = np.ascontiguousarray(np.broadcast_to(np.tile(mk, (2, 1))[:, None, :], (128, 16, 64))).astype(np.float32)
    i = np.arange(128)[:, None]; j = np.arange(128)[None, :]
    prev = np.where(i >= j, 0.0, NEG); nxt = np.where(i <= j, 0.0, NEG)
    c["swamask"] = _bf(np.stack([prev, np.zeros((128, 128)), nxt], axis=1))
    return c


ROPE_PERM = np.array([(d + 16) if (d % 32) < 16 else (d - 16) for d in range(64)])
SWA_QORDER = [0, 2, 1, 3]


VG = 0
VB = 48
VCW = 120
VCB = 182
VLG = 184
VLB = 186
VSK = 188
VL = 192
NV = 2 * VL + 24


def _host_inputs(inp):
    f = lambda a: np.ascontiguousarray(np.asarray(a, dtype=np.float32))
    x = f(inp["x"]); ctx = f(inp["ctx"]); c = f(inp["c"]); c_ctx = f(inp["c_ctx"])
    shared = {}
    shared["w_ada"] = f(inp["w_ada"])
    shared["w1"] = f(inp["ffn_w1"]); shared["w3"] = f(inp["ffn_w3"]); shared["w2"] = f(inp["ffn_w2"])
    w_in = f(inp["w_in"])
    qs0 = 512 + 256 + 768
    qs_cols = np.concatenate([qs0 + h * 64 + np.arange(64) for h in SWA_QORDER])
    qsp_cols = np.concatenate([qs0 + h * 64 + ROPE_PERM for h in SWA_QORDER])
    ks0 = qs0 + 256
    ks_cols = ks0 + np.arange(128)
    ksp_cols = np.concatenate([ks0 + h * 64 + ROPE_PERM for h in range(2)])
    vs_cols = ks0 + 128 + np.arange(128)
    cols = np.concatenate([np.arange(0, qs0), qs_cols, ks_cols, vs_cols, qsp_cols, ksp_cols])
    shared["w_in"] = np.ascontiguousarray(w_in[:, :, cols])
    w_out = f(inp["w_out"])
    rows = np.concatenate([np.arange(0, 768)] + [768 + h * 64 + np.arange(64) for h in SWA_QORDER])
    shared["w_out"] = np.ascontiguousarray(w_out[:, rows, :])
    rb = f(inp["nat_rel_bias"])
    kc = np.arange(64)[:, None]; qc = np.arange(64)[None, :]
    dc = np.clip(kc - qc + 15, 0, 30)
    tm = rb[:, :, :, dc]
    tm = np.concatenate([tm, tm[:, :, 14:15]], axis=2)
    lo = tm.transpose(0, 3, 1, 2, 4)
    hi = np.concatenate([tm[:, :, 1:], tm[:, :, 15:16]], axis=2).transpose(0, 3, 1, 2, 4)
    shared["natlib"] = np.ascontiguousarray(np.concatenate([lo, hi], axis=1))
    shared.update(_consts())
    vec = np.zeros((128, NV), np.float32)
    for l in range(DEPTH):
        o = l * VL
        vec[:, o + VG:o + VG + 48] = f(inp["norm_g"])[l].reshape(6, 8, 128).transpose(2, 0, 1).reshape(128, 48)
        vec[:, o + VB:o + VB + 72] = f(inp["b_ada"])[l].reshape(72, 128).T
        vec[:, o + VCW:o + VCW + 62] = f(inp["conv_w"])[l].reshape(31, 2, 128).transpose(2, 1, 0).reshape(128, 62)
        vec[:, o + VCB:o + VCB + 2] = f(inp["conv_b"])[l].reshape(2, 128).T
        vec[:, o + VLG:o + VLG + 2] = f(inp["conv_ln_g"])[l].reshape(2, 128).T
        vec[:, o + VLB:o + VLB + 2] = f(inp["conv_ln_b"])[l].reshape(2, 128).T
        vec[:, o + VSK:o + VSK + 4] = np.broadcast_to(f(inp["sink_logits"])[l][None, :], (128, 4))
    maps = []
    for core in range(8):
        b0 = 2 * core
        m = dict(shared)
        m["xT"] = np.ascontiguousarray(x[b0:b0 + 2].transpose(0, 2, 1))
        m["cxT"] = np.ascontiguousarray(ctx[b0:b0 + 2].transpose(0, 2, 1))
        v = vec.copy()
        cc = np.stack([c[b0], c[b0 + 1], c_ctx], axis=1)
        v[:, 2 * VL:] = cc.reshape(8, 128, 3).transpose(1, 0, 2).reshape(128, 24)
        m["vec"] = v
        maps.append(m)
    return maps


def build(stop_after=None, dbg=None):
    DBG = dbg or {}
    nc = bass.Bass("TRN2", target_bir_lowering=False)
    stack = ExitStack()
    P = Prog(nc, stack)

    def din(name, shape, dt=F32):
        return nc.dram_tensor(name, list(shape), dt, kind="ExternalInput").ap()
    xT = din("xT", [2, D, NL]); cxT = din("cxT", [2, D, NC_])
    w_ada = din("w_ada", [DEPTH, D, 9 * D])
    w1 = din("w1", [DEPTH, 2, D, FF]); w3 = din("w3", [DEPTH, 2, D, FF]); w2 = din("w2", [DEPTH, 2, FF, D])
    NIN = 2432
    w_in = din("w_in", [DEPTH, D, NIN]); w_out = din("w_out", [DEPTH, D, D])
    natlib = din("natlib", [DEPTH, 128, 4, 16, 64])
    vec_d = din("vec", [128, NV])
    bd_d = din("bd", [128, 256], BF16); dftl_d = din("dftl", [16, 128, 16, 2, 128], BF16)
    dftc_d = din("dftc", [128, 2, 2, 256], BF16)
    ident_d = din("ident", [128, 128], BF16)
    rope_d = din("rope", [128, 2, NL]); natmask_d = din("natmask", [128, 16, 64]); swamask_d = din("swamask", [128, 3, 128], BF16)
    outT = nc.dram_tensor("outT", [2, D, NL], F32, kind="ExternalOutput").ap()
    dbg_out = nc.dram_tensor("dbg", [D, T], BF16, kind="ExternalOutput").ap() if DBG else None

    X = P.sb([128, 8, T], F32)
    VEC = P.sb([128, NV], F32)
    MOD = P.sb([128, DEPTH, 72, 3], F32)
    RSTD = P.sb([128, T], F32)
    ONES = P.sb([128, 128], BF16)
    COEF = P.sb([128, 3, 3, 2, 8], F32)
    SCT = P.sb([128, 24], BF16)
    ESINK = P.sb([128, 4], F32)
    IDNP = P.sb([128, 128], BF16)
    ESROW = P.sb([1, 512], BF16)
    P.dma("sp", VEC[:], vec_d)
    P.memset("dve", ONES[:], 1.0)
    P.dma("sp", IDNP[:], ident_d)

    def vcol(l, off, n=1):
        return VEC[:, l * VL + off: l * VL + off + n]

    MWB = [P.sb([128, 8, 128], BF16) for _ in range(2)]
    MCNT = {"n": 0}

    def mod_chunk(l, j):
        def thunk():
            wb = MWB[MCNT["n"] % 2]; MCNT["n"] += 1
            P.dma("pool", wb[:], w_ada[l].rearrange("(c p) n -> p c n", p=128)[:, :, j * 128:(j + 1) * 128])
            ps = P.ps()
            for k in range(8):
                P.mm(ps[:, 0:3], wb[:, k, :], SCT[:, 3 * k:3 * k + 3], start=(k == 0), stop=(k == 7))
            P.ts("dve", MOD[:, l, j, :], ps[:, 0:3], vcol(l, VB + j), None, ALU.add)
        return thunk

    def bg_step(bg, n=1):
        for _ in range(n):
            if bg:
                bg.pop(0)()

    def stage_coef(l, bi, subl=(0, 1, 2)):
        m = P.mark()
        for s in subl:
            gpre = vcol(l, VG + (2 * s) * 8, 8); gpost = vcol(l, VG + (2 * s + 1) * 8, 8)
            for w, col in ((0, bi), (1, 2)):
                shift = MOD[:, l, (3 * s) * 8:(3 * s) * 8 + 8, col]
                scale = MOD[:, l, (3 * s + 1) * 8:(3 * s + 1) * 8 + 8, col]
                gate = MOD[:, l, (3 * s + 2) * 8:(3 * s + 2) * 8 + 8, col]
                P.stt("dve", COEF[:, s, 0, w, :], scale, 1.0, gpre, ALU.add, ALU.mult)
                P.copy("dve", COEF[:, s, 1, w, :], shift)
                P.stt("dve", COEF[:, s, 2, w, :], gate, (1.0 if s == 1 else 0.5), gpost, ALU.mult, ALU.mult)
        P.release(m)

    def rsqrt_to(out, in_, scale):
        P.act(out, in_, AF.Ln, bias=EPS, scale=scale)
        P.act(out, out, AF.Exp, scale=-0.5)

    def recip_to(out, in_, bias=0.0):
        P.act(out, in_, AF.Ln, bias=bias)
        P.act(out, out, AF.Exp, scale=-1.0)

    def segs(t0, t1):
        out = []
        if t0 < NL:
            out.append((t0, min(t1, NL), 0))
        if t1 > NL:
            out.append((max(t0, NL), t1, 1))
        return out

    def blocks(t0, t1, tb=TB):
        return [(a, min(a + tb, t1)) for a in range(t0, t1, tb)]

    def stage_rstd(src, soff, t0, t1, dst, doff, nfeat=D, nch=8, SQ=None):
        m = P.mark()
        SQ = SQ or [P.sb([128, TB], BF16) for _ in range(3)]
        it = 0
        for (a, b) in blocks(t0, t1):
            n = b - a
            ps = P.ps()
            for c in range(nch):
                sq = SQ[it % len(SQ)]; it += 1
                s = src[:, c, soff + a - t0: soff + b - t0]
                if c % 2 == 0:
                    P.act(sq[:, 0:n], s, AF.Square)
                else:
                    P.tt("dve", sq[:, 0:n], s, s, ALU.mult)
                P.mm(ps[:, 0:n], ONES[:, :], sq[:, 0:n], start=(c == 0), stop=(c == nch - 1))
            d = dst[:, doff + a - t0: doff + b - t0]
            rsqrt_to(d, ps[:, 0:n], 1.0 / nfeat)
        P.release(m)

    def stage_prenorm(s, H, hoff, t0, t1, TMP=None):
        m = P.mark()
        TMP = TMP or [P.sb([128, TB], F32) for _ in range(4)]
        it = 0
        for (a0, b0) in blocks(t0, t1):
            for (a, b, w) in segs(a0, b0):
                n = b - a
                for c in range(8):
                    tmp = TMP[it % len(TMP)]; it += 1
                    P.stt("dve", tmp[:, 0:n], X[:, c, a:b], COEF[:, s, 0, w, c:c + 1], RSTD[:, a:b], ALU.mult, ALU.mult)
                    P.act(H[:, c, hoff + a - t0: hoff + b - t0], tmp[:, 0:n], AF.Identity, bias=COEF[:, s, 1, w, c:c + 1])
        P.release(m)

    def stage_postres(s, Y, yoff, t0, t1, RS, roff, TMP=None):
        m = P.mark()
        TMP = TMP or [P.sb([128, TB], F32) for _ in range(4)]
        it = 0
        for (a0, b0) in blocks(t0, t1):
            for (a, b, w) in segs(a0, b0):
                n = b - a
                for c in range(8):
                    tmp = TMP[it % len(TMP)]; it += 1
                    P.stt("dve", tmp[:, 0:n], Y[:, c, yoff + a - t0: yoff + b - t0], COEF[:, s, 2, w, c:c + 1],
                          RS[:, roff + a - t0: roff + b - t0], ALU.mult, ALU.mult)
                    P.tt("dve", X[:, c, a:b], X[:, c, a:b], tmp[:, 0:n], ALU.add)
        P.release(m)

    def stage_rstd_big(t0, t1, SQB):
        for c in range(8):
            P.act(SQB[:, c, t0:t1], X[:, c, t0:t1], AF.Square)
        for (a, b) in blocks(t0, t1):
            ps = P.ps()
            for c in range(8):
                P.mm(ps[:, 0:b - a], ONES[:, :], SQB[:, c, a:b], start=(c == 0), stop=(c == 7))
            rsqrt_to(RSTD[:, a:b], ps[:, 0:b - a], 1.0 / D)

    def stage_prenorm_big(s, H, hoff, t0, t1, scr):
        for c in range(8):
            for (a, b, w) in segs(t0, t1):
                tmp = scr[c % len(scr)][:, a - t0:b - t0]
                P.stt("dve", tmp, X[:, c, a:b], COEF[:, s, 0, w, c:c + 1], RSTD[:, a:b], ALU.mult, ALU.mult)
                P.act(H[:, c, hoff + a - t0: hoff + b - t0], tmp, AF.Identity, bias=COEF[:, s, 1, w, c:c + 1])

    def stage_postres_big(s, Y, t0, t1):
        for c in range(8):
            for (a, b, w) in segs(t0, t1):
                y = Y[:, c, a - t0:b - t0]
                P.stt("dve", y, y, COEF[:, s, 2, w, c:c + 1], RSTD[:, a:b], ALU.mult, ALU.mult)
                P.tt("pool" if c >= 3 else "dve", X[:, c, a:b], X[:, c, a:b], y, ALU.add)

    def stage_ffn(l, which, s, tend, bg=None, after_mid=None):
        bg = bg if bg is not None else []
        GS = 1152
        m = P.mark()
        Yb = P.sb([128, 8, GS], F32)
        Hb = P.sb([128, 8, GS], BF16, at=Yb.base)
        SQB = P.sb([128, 8, T], BF16, at=Yb.base)
        SCR = [P.sb([128, GS], F32, at=Yb.base + (4 + i) * GS * 4) for i in range(4)]
        stage_rstd_big(0, tend, SQB)
        G = P.sb([128, NFC, GS], BF16)
        NW = 3
        W13 = [(P.sb([128, 8, 128], BF16), P.sb([128, 8, 128], BF16)) for _ in range(NW)]
        HF = NFC // 2
        NW2 = 4
        W2B = [P.sb([128, HF, 128], BF16) for _ in range(NW2)]
        SA = [P.sb([128, TB], F32) for _ in range(2)]
        SQ = [P.sb([128, TB], BF16) for _ in range(3)]
        w1s = w1[l, which].rearrange("(c p) n -> p c n", p=128)
        w3s = w3[l, which].rearrange("(c p) n -> p c n", p=128)
        w2s = w2[l, which].rearrange("(c p) n -> p c n", p=128)
        groups = [(g0, min(g0 + GS, tend)) for g0 in range(0, tend, GS)]
        cnt = {"a": 0, "b": 0, "c": 0, "q": 0}
        pre = {}

        def issue_w13(f):
            wa, wb = W13[cnt["a"] % NW]; cnt["a"] += 1
            P.dma("pool", wa[:], w1s[:, :, f * 128:(f + 1) * 128])
            P.dma("pool", wb[:], w3s[:, :, f * 128:(f + 1) * 128])
            return wa, wb
        for f in range(NW):
            pre[f] = issue_w13(f)
        stage_prenorm_big(s, Hb, 0, groups[0][0], groups[0][1], SCR)
        for gi, (g0, g1) in enumerate(groups):
            blks = blocks(g0, g1)
            for f in range(NFC):
                if f in pre:
                    wa, wb = pre.pop(f)
                else:
                    wa, wb = issue_w13(f)
                bg_step(bg, 1)
                for (a, b) in blks:
                    n = b - a
                    pa = P.ps(); pb = P.ps()
                    for k in range(8):
                        P.mm(pa[:, 0:n], wa[:, k, :], Hb[:, k, a - g0:b - g0], start=(k == 0), stop=(k == 7))
                    for k in range(8):
                        P.mm(pb[:, 0:n], wb[:, k, :], Hb[:, k, a - g0:b - g0], start=(k == 0), stop=(k == 7))
                    sa = SA[cnt["b"] % 2]; cnt["b"] += 1
                    P.act(sa[:, 0:n], pa[:, 0:n], AF.Silu)
                    P.tt("dve", G[:, f, a - g0:b - g0], pb[:, 0:n], sa[:, 0:n], ALU.mult)
            acc = P.ps_reserve(len(blks))
            pend = []
            for dc in range(8):
                halves = []
                for hh in range(2):
                    w2b = W2B[cnt["c"] % NW2]; cnt["c"] += 1
                    P.dma("pool", w2b[:], w2s[:, hh * HF:(hh + 1) * HF, dc * 128:(dc + 1) * 128])
                    halves.append(w2b)
                bg_step(bg, 2)
                for bi_, (a, b) in enumerate(blks):
                    n = b - a
                    py = P.ps()
                    for f in range(NFC):
                        P.mm(py[:, 0:n], halves[f // HF][:, f % HF, :], G[:, f, a - g0:b - g0], start=(f == 0), stop=(f == NFC - 1))
                    sq = SQ[cnt["q"] % 3]; cnt["q"] += 1
                    P.act(sq[:, 0:n], py[:, 0:n], AF.Square)
                    for (x0, x1, w) in segs(a, b):
                        P.act(Yb[:, dc, x0 - g0:x1 - g0], py[:, x0 - a:x1 - a], AF.Identity, scale=COEF[:, s, 2, w, dc:dc + 1])
                    pend.append((bi_, n, sq, dc))
                    if len(pend) > 2:
                        pb_, pn, psq, pdc = pend.pop(0)
                        P.mm(acc[pb_][:, 0:pn], ONES[:, :], psq[:, 0:pn], start=(pdc == 0), stop=(pdc == 7))
            for (pb_, pn, psq, pdc) in pend:
                P.mm(acc[pb_][:, 0:pn], ONES[:, :], psq[:, 0:pn], start=(pdc == 0), stop=(pdc == 7))
            for bi_, (a, b) in enumerate(blks):
                rsqrt_to(RSTD[:, a:b], acc[bi_][:, 0:b - a], 1.0 / D)
            P.ps_release(acc)
            if gi + 1 < len(groups):
                for f in range(NW):
                    pre[f] = issue_w13(f)
            for c in range(8):
                y = Yb[:, c, 0:g1 - g0]
                P.tt("dve", y, y, RSTD[:, g0:g1], ALU.mult)
                P.tt("pool" if c >= 4 else "dve", X[:, c, g0:g1], X[:, c, g0:g1], y, ALU.add)
            if gi + 1 < len(groups):
                stage_prenorm_big(s, Hb, 0, groups[gi + 1][0], groups[gi + 1][1], SCR)
                if after_mid is not None:
                    after_mid(g0, g1)
        P.release(m)

    def ps3(bank, c0, c1, inner):
        return View(bank.h[:, c0:c1].rearrange("p (a b) -> p a b", b=inner), bank[:, c0:c1].rects)

    CQ = {"n": 0}

    def evac_s(out, in_, sc):
        CQ["n"] += 1
        if CQ["n"] % 2:
            P.act(out, in_, AF.Identity, scale=sc)
        else:
            P.ts("dve", out, in_, sc, None, ALU.mult)

    def skew(stA, stB, lag):
        n = len(stA)
        for i in range(n + lag):
            if i < n:
                stA[i]()
            if i - lag >= 0:
                stB[i - lag]()

    def evac(out, in_):
        CQ["n"] += 1
        P.copy("act" if CQ["n"] % 2 else "dve", out, in_)

    def proj_fm(l, Hb, col0, ncols, tend, consume):
        m = P.mark()
        W = P.sb([128, 8, ncols], BF16)
        P.dma("pool", W[:], w_in[l].rearrange("(c p) n -> p c n", p=128)[:, :, col0:col0 + ncols])
        for j in range(ncols // 128):
            for (a, b) in blocks(0, tend):
                ps = P.ps()
                for k in range(8):
                    P.mm(ps[:, 0:b - a], W[:, k, j * 128:(j + 1) * 128], Hb[:, k, a:b], start=(k == 0), stop=(k == 7))
                consume(j, a, b, ps)
        P.release(m)

    def proj_tm(l, Hb, col0, ncols, tiles, dst, wbuf=None):
        m = P.mark()
        W = wbuf if wbuf is not None else P.sb([128, 8, ncols], BF16)
        P.dma("pool", W[:], w_in[l].rearrange("(c p) n -> p c n", p=128)[:, :, col0:col0 + ncols])
        for i, t0 in enumerate(tiles):
            ps = P.ps()
            for k in range(8):
                P.mm(ps[:, 0:ncols], Hb[:, k, t0:t0 + 128], W[:, k, :], start=(k == 0), stop=(k == 7))
            evac(dst[:, i, :], ps[:, 0:ncols])
        P.release(m)

    def stage_mix(l, bi, last):
        s = 1
        tq = NL if last else T
        m0 = P.mark()
        YM = P.sb([128, 8, T], BF16)
        H = P.sb([128, 8, T], BF16)
        mq = P.mark()
        SQB = P.sb([128, 8, T], BF16)
        SCRM = [P.sb([128, T], F32, at=SQB.base + i * T * 4) for i in range(4)]
        stage_rstd_big(0, T, SQB)
        stage_prenorm_big(s, H, 0, 0, T, SCRM)
        P.release(mq)
        SCR = RSTD.base

        def scr(shape, dtype, off):
            return P.sb(shape, dtype, at=SCR + off)

        m = P.mark()
        LA = 15 + NL + 30 + NC_ + 15
        OC = 15 + NL + 30
        A = P.sb([128, 2, LA], BF16)
        for cc in range(2):
            P.memset("pool", A[:, cc, 0:15], 0.0)
            P.memset("pool", A[:, cc, 15 + NL:OC], 0.0)
            P.memset("pool", A[:, cc, LA - 15:LA], 0.0)
        DIAG = P.sb([128, 62, 128], BF16)
        IDN = IDNP
        dq = list(range(62))

        def diag_some(k):
            for _ in range(k):
                if dq:
                    i = dq.pop(0)
                    if i % 2 == 0:
                        P.ts("dve", DIAG[:, i, :], IDN[:], vcol(l, VCW + i), None, ALU.mult)
                    else:
                        P.act(DIAG[:, i, :], IDN[:], AF.Identity, scale=vcol(l, VCW + i))
        SG = [scr([128, TB], F32, i * TB * 4) for i in range(3)]
        m1 = P.mark()
        W = P.sb([128, 8, 512], BF16)
        P.dma("pool", W[:], w_in[l].rearrange("(c p) n -> p c n", p=128)[:, :, 0:512])
        it = 0
        for cc in range(2):
            for (a, b) in blocks(0, tq):
                n = b - a
                p1 = P.ps(); p2 = P.ps()
                for k in range(8):
                    P.mm(p1[:, 0:n], W[:, k, cc * 128:(cc + 1) * 128], H[:, k, a:b], start=(k == 0), stop=(k == 7))
                for k in range(8):
                    P.mm(p2[:, 0:n], W[:, k, 256 + cc * 128:256 + (cc + 1) * 128], H[:, k, a:b], start=(k == 0), stop=(k == 7))
                sg = SG[it % 3]; it += 1
                P.act(sg[:, 0:n], p2[:, 0:n], AF.Sigmoid)
                for (sa, sb_, w) in segs(a, b):
                    d0 = (15 + sa) if w == 0 else (OC + sa - NL)
                    P.tt("dve", A[:, cc, d0:d0 + (sb_ - sa)], p1[:, sa - a:sb_ - a], sg[:, sa - a:sb_ - a], ALU.mult)
                diag_some(7)
        diag_some(62)
        P.release(m1)
        ACCR = [P.sb([128, 2, TB], F32) for _ in range(3)]
        ABP = [P.sb([128, TB], BF16) for _ in range(8)]
        MEAN = scr([128, TB], F32, 0); VAR = scr([128, TB], F32, TB * 4); NMR = scr([128, TB], F32, 2 * TB * 4)
        Z = [scr([128, TB], F32, (3 + i) * TB * 4) for i in range(2)]
        MSQ = scr([128, TB], F32, 5 * TB * 4)
        regs = [(a, b, 0) for (a, b) in blocks(0, NL)] + ([] if last else [(a, b, OC - 15 - NL) for (a, b) in blocks(NL, T)])
        stA = []; stB = []
        for ri, (a, b, ib) in enumerate(regs):
            def mk(ri=ri, a=a, b=b, ib=ib):
                n = b - a
                acc = ACCR[ri % 3]
                tiles = [ABP[(ri % 2) * 4 + q] for q in range(4)]

                def A_():
                    for cc in range(2):
                        ps = P.ps()
                        for j in range(31):
                            P.mm(ps[:, 0:n], DIAG[:, cc * 31 + j, :], A[:, cc, ib + a + j: ib + a + j + n], start=(j == 0), stop=(j == 30))
                        P.act(acc[:, cc, 0:n], ps[:, 0:n], AF.Identity, bias=vcol(l, VCB + cc))
                        P.act(tiles[2 * cc + 1][:, 0:n], ps[:, 0:n], AF.Square, bias=vcol(l, VCB + cc))
                        P.copy("dve", tiles[2 * cc][:, 0:n], acc[:, cc, 0:n])

                def B_():
                    pS = P.ps(); pQ = P.ps()
                    for cc in range(2):
                        P.mm(pS[:, 0:n], ONES[:, :], tiles[2 * cc][:, 0:n], start=(cc == 0), stop=(cc == 1))
                        P.mm(pQ[:, 0:n], ONES[:, :], tiles[2 * cc + 1][:, 0:n], start=(cc == 0), stop=(cc == 1))
                    P.act(MEAN[:, 0:n], pS[:, 0:n], AF.Identity, scale=1.0 / 256)
                    P.tt("dve", MSQ[:, 0:n], MEAN[:, 0:n], MEAN[:, 0:n], ALU.mult)
                    P.stt("dve", VAR[:, 0:n], pQ[:, 0:n], 1.0 / 256, MSQ[:, 0:n], ALU.mult, ALU.subtract)
                    rsqrt_to(VAR[:, 0:n], VAR[:, 0:n], 1.0)
                    P.stt("dve", NMR[:, 0:n], MEAN[:, 0:n], -1.0, VAR[:, 0:n], ALU.mult, ALU.mult)
                    for cc in range(2):
                        z = Z[cc]
                        P.tt("dve", z[:, 0:n], acc[:, cc, 0:n], VAR[:, 0:n], ALU.mult)
                        P.tt("dve", z[:, 0:n], z[:, 0:n], NMR[:, 0:n], ALU.add)
                        P.act(YM[:, cc, a:b], z[:, 0:n], AF.Silu, bias=vcol(l, VLB + cc), scale=vcol(l, VLG + cc))
                return A_, B_
            a_, b_ = mk()
            stA.append(a_); stB.append(b_)
        skew(stA, stB, 1)
        P.release(m)

        m = P.mark()
        UBf = P.sb([128, 2, T], BF16)
        BD = P.sb([128, 256], BF16)
        P.dma("sp", BD[:], bd_d)
        proj_fm(l, H, 512, 256, tq, lambda j, a, b, ps: evac(UBf[:, j, a:b], ps[:, 0:b - a]))
        PQ = P.sb([128, 2, 18, 256], BF16)
        ntile = 16 if last else 18
        for cc in range(2):
            for t in range(ntile):
                ps = P.ps()
                P.mm(ps[:, 0:256], UBf[:, cc, t * 128:(t + 1) * 128], BD[:, :])
                evac(PQ[:, cc, t, :], ps[:, 0:256])
        CS = [P.sb([128, 16, 2, 128], BF16), P.sb([128, 16, 2, 128], BF16, at=UBf.base)]
        for nb in range(16):
            cs = CS[nb % 2]
            P.dma("sp", cs[:], dftl_d[nb])
            for cc in range(2):
                ps = P.ps()
                for t in range(16):
                    P.mm(ps[:, 0:128], PQ[:, cc, t, 0:128], cs[:, t, 0, :], start=(t == 0), stop=False)
                    P.mm(ps[:, 0:128], PQ[:, cc, t, 128:256], cs[:, t, 1, :], start=False, stop=(t == 15))
                evac(YM[:, 2 + cc, nb * 128:(nb + 1) * 128], ps[:, 0:128])
        if not last:
            CSC = P.sb([128, 2, 2, 256], BF16)
            P.dma("sp", CSC[:], dftc_d)
            for cc in range(2):
                ps = P.ps()
                for t in range(2):
                    P.mm(ps[:, 0:256], PQ[:, cc, 16 + t, 0:128], CSC[:, t, 0, :], start=(t == 0), stop=False)
                    P.mm(ps[:, 0:256], PQ[:, cc, 16 + t, 128:256], CSC[:, t, 1, :], start=False, stop=(t == 1))
                evac(YM[:, 2 + cc, NL:T], ps[:, 0:256])
        P.release(m)

        m = P.mark()
        LIB = P.sb([128, 4, 16, 64], BF16)
        RAW = scr([128, 16, 64], F32, 0); MT = scr([128, 16, 64], F32, 4096)
        P.dma("sp", MT[:], natmask_d)
        for h in range(4):
            P.dma("sp", RAW[:], natlib[l, :, h])
            P.tt("dve", LIB[:, h], RAW[:], MT[:], ALU.add)
        QK = P.sb([128, 4, T], BF16)
        proj_fm(l, H, 768, 512, T, lambda j, a, b, ps: (evac_s(QK[:, j, a:b], ps[:, 0:b - a], 0.125) if j < 2
                                                      else evac(QK[:, j, a:b], ps[:, 0:b - a])))
        VNa = P.sb([128, 18, 256], BF16)
        VNb = P.sb([128, 15, 256], BF16)
        WVS = scr([128, 8, 256], BF16, 0)
        proj_tm(l, H, 1280, 256, [t * 128 for t in range(18)], VNa, wbuf=WVS)
        proj_tm(l, H, 1280, 256, [64 + t * 128 for t in range(15)], VNb, wbuf=WVS)
        TMPS = [scr([128, 4, 64], F32, i * 1024) for i in range(3)]
        PTS = [scr([128, 6, 64], BF16, 3072 + i * 768) for i in range(3)]
        RCS = [scr([128, 64], F32, 5376 + i * 256) for i in range(3)]
        it = 0
        stA = []; stB = []
        for h in range(4):
            for r in range(ROWS):
                def mk(h=h, r=r, i=it):
                    hc = h // 2; po = (h % 2) * 64
                    rs = min(max(r - 4, 0), ROWS - 8); dr0 = rs - r + 7; q0 = r * 64; k0 = rs * 64
                    tmp = TMPS[i % 3]; pt = PTS[i % 3]; rc = RCS[i % 3]

                    def A():
                        pS = P.ps()
                        qv = QK[po:po + 64, hc, q0:q0 + 64]
                        for j in range(4):
                            P.mm(pS[:, j * 64:(j + 1) * 64], QK[po:po + 64, 2 + hc, k0 + 128 * j:k0 + 128 * j + 128], qv,
                                 start=(j == 0), stop=False)
                        for j in range(2):
                            P.mm(pS[:, (4 + j) * 64:(5 + j) * 64], QK[po:po + 64, 2 + hc, NL + 128 * j:NL + 128 * j + 128], qv,
                                 start=False, stop=False)
                        libv = View(LIB.h[:, h, dr0:dr0 + 8:2, :], LIB[:, h, dr0:dr0 + 8, :].rects)
                        P.mm(ps3(pS, 0, 256, 64), IDNP[:, :], libv, start=False, stop=True)
                        P.act(pt[:], ps3(pS, 0, 384, 64), AF.Exp)

                    def B():
                        pO = P.ps(); pR = P.ps()
                        for j in range(6):
                            if j < 4:
                                ks = k0 + 128 * j
                                vt = VNa[:, ks // 128, hc * 128:(hc + 1) * 128] if ks % 128 == 0 else VNb[:, (ks - 64) // 128, hc * 128:(hc + 1) * 128]
                            else:
                                vt = VNa[:, 16 + (j - 4), hc * 128:(hc + 1) * 128]
                            P.mm(pO[:, 0:64], vt, pt[:, j, :], start=(j == 0), stop=(j == 5))
                        for j in range(6):
                            P.mm(pR[:, 0:64], ONES[:, :], pt[:, j, :], start=(j == 0), stop=(j == 5))
                        P.recip(rc[po:po + 64, :], pR[po:po + 64, 0:64])
                        P.tt("dve", YM[po:po + 64, 4 + hc, q0:q0 + 64], pO[po:po + 64, 0:64], rc[po:po + 64, :], ALU.mult)
                    return A, B
                a_, b_ = mk()
                stA.append(a_); stB.append(b_); it += 1
        skew(stA, stB, 2)
        if not last:
            PTC = scr([128, 512], BF16, 0); RCC = scr([128, 256], F32, 1024)
            for h in range(4):
                hc = h // 2; po = (h % 2) * 64
                pS = P.ps()
                for j in range(2):
                    P.mm(pS[:, j * 256:(j + 1) * 256], QK[po:po + 64, 2 + hc, NL + 128 * j:NL + 128 * j + 128], QK[po:po + 64, hc, NL:T])
                P.act(PTC[:], pS[:, 0:512], AF.Exp)
                pO = P.ps(); pR = P.ps()
                for j in range(2):
                    P.mm(pO[:, 0:256], VNa[:, 16 + j, hc * 128:(hc + 1) * 128], PTC[:, j * 256:(j + 1) * 256], start=(j == 0), stop=(j == 1))
                for j in range(2):
                    P.mm(pR[:, 0:256], ONES[:, :], PTC[:, j * 256:(j + 1) * 256], start=(j == 0), stop=(j == 1))
                recip_to(RCC[po:po + 64, :], pR[po:po + 64, 0:256])
                P.tt("dve", YM[po:po + 64, 4 + hc, NL:T], pO[po:po + 64, 0:256], RCC[po:po + 64, :], ALU.mult)
        P.release(m)

        m = P.mark()
        QS = P.sb([128, 2, T], BF16)
        KSb = P.sb([128, T], BF16)
        VSa = P.sb([128, 18, 128], BF16)
        SM = P.sb([128, 3, 128], BF16)
        P.dma("sp", SM[:], swamask_d)
        P.act(ESINK[:], vcol(l, VSK, 4), AF.Exp)
        for hh in range(4):
            P.act(ESROW[0:1, hh * 128:(hh + 1) * 128], ONES[0:1, :], AF.Identity, bias=ESINK[0:1, hh:hh + 1], scale=0.0)
        proj_tm(l, H, 1920, 128, [t * 128 for t in range(18)], VSa)
        m1 = P.mark()
        WQ = P.sb([128, 8, 256], BF16); WQP = P.sb([128, 8, 256], BF16)
        WK = P.sb([128, 8, 128], BF16); WKP = P.sb([128, 8, 128], BF16)
        wv = w_in[l].rearrange("(c p) n -> p c n", p=128)
        P.dma("pool", WQ[:], wv[:, :, 1536:1792]); P.dma("pool", WQP[:], wv[:, :, 2048:2304])
        P.dma("pool", WK[:], wv[:, :, 1792:1920]); P.dma("pool", WKP[:], wv[:, :, 2304:2432])
        RT = [P.sb([128, 2, TB], F32) for _ in range(2)]
        T1 = [scr([128, TB], F32, i * TB * 4) for i in range(3)]
        T2 = [scr([128, TB], F32, (3 + i) * TB * 4) for i in range(3)]
        it = 0; itr = 0
        for (a, b) in blocks(0, T):
            n = b - a
            rt = None
            if a < NL:
                rt = RT[itr % 2]; itr += 1
                nl_ = min(b, NL) - a
                P.dma("sp", rt[:, :, 0:nl_], rope_d[:, :, a:a + nl_])
            for (wa, wp, dst, qsc) in ((WQ[:, :, 0:128], WQP[:, :, 0:128], lambda x0, x1: QS[:, 0, x0:x1], 0.125),
                                       (WQ[:, :, 128:256], WQP[:, :, 128:256], lambda x0, x1: QS[:, 1, x0:x1], 0.125),
                                       (WK[:, :, :], WKP[:, :, :], lambda x0, x1: KSb[:, x0:x1], 1.0)):
                p1 = P.ps(); p2 = P.ps()
                wa_h, wp_h = wa, wp
                for k in range(8):
                    P.mm(p1[:, 0:n], View(wa_h.ap[:, k, :], wa_h.rects), H[:, k, a:b], start=(k == 0), stop=(k == 7))
                if a < NL:
                    for k in range(8):
                        P.mm(p2[:, 0:n], View(wp_h.ap[:, k, :], wp_h.rects), H[:, k, a:b], start=(k == 0), stop=(k == 7))
                for (sa, sb_, w) in segs(a, b):
                    nn = sb_ - sa
                    if w == 0:
                        t1 = T1[it % 3]; t2 = T2[it % 3]; it += 1
                        P.stt("dve", t1[:, 0:nn], p1[:, sa - a:sb_ - a], qsc, rt[:, 0, sa - a:sb_ - a], ALU.mult, ALU.mult)
                        P.stt("dve", t2[:, 0:nn], p2[:, sa - a:sb_ - a], qsc, rt[:, 1, sa - a:sb_ - a], ALU.mult, ALU.mult)
                        P.tt("pool", dst(sa, sb_), t1[:, 0:nn], t2[:, 0:nn], ALU.add)
                    else:
                        evac_s(dst(sa, sb_), p1[:, sa - a:sb_ - a], qsc)
        P.release(m1)
        WO = P.sb([128, 8, D], BF16, at=H.base + 8 * T * 2 + 24576)
        P.dma("pool", WO[:], w_out[l].rearrange("(c p) n -> p c n", p=128))
        TMP2 = [scr([128, 3, 128], F32, i * 1536) for i in range(3)]
        PT2 = [scr([128, 5, 128], BF16, 4608 + i * 1280) for i in range(3)]
        RC2 = [P.sb([128, 128], F32) for _ in range(3)]
        it = 0
        stA = []; stB = []
        for sl in range(4):
            for qb in range(16):
                def mk(sl=sl, qb=qb, i=it):
                    h = SWA_QORDER[sl]; ch = sl // 2; po = (sl % 2) * 64
                    q0 = qb * 128
                    chunks = ([(qb - 1, 0)] if qb > 0 else []) + [(qb, 1)] + ([(qb + 1, 2)] if qb < 15 else [])
                    nl_ = len(chunks)
                    tmp = TMP2[i % 3]; pt = PT2[i % 3]; rc = RC2[i % 3]

                    def A():
                        pL = P.ps(); pC = P.ps()
                        qv = QS[po:po + 64, ch, q0:q0 + 128]
                        for ii, (kb, mi) in enumerate(chunks):
                            P.mm(pL[:, ii * 128:(ii + 1) * 128], KSb[po:po + 64, kb * 128:(kb + 1) * 128], qv,
                                 start=(ii == 0), stop=False)
                        m0_ = chunks[0][1]
                        P.mm(ps3(pL, 0, nl_ * 128, 128), IDNP[:, :], SM[:, m0_:m0_ + nl_, :], start=False, stop=True)
                        for j in range(2):
                            P.mm(pC[:, j * 128:(j + 1) * 128], KSb[po:po + 64, NL + 128 * j:NL + 128 * j + 128], qv,
                                 start=(j == 0), stop=(j == 1))
                        P.act(pt[:, 0:nl_, :], ps3(pL, 0, nl_ * 128, 128), AF.Exp)
                        P.act(pt[:, nl_:nl_ + 2, :], ps3(pC, 0, 256, 128), AF.Exp)

                    def B():
                        pO = P.ps(); pR = P.ps()
                        vts = [VSa[:, kb, :] for (kb, mi) in chunks] + [VSa[:, 16, :], VSa[:, 17, :]]
                        nk = nl_ + 2
                        for ii in range(nk):
                            P.mm(pO[:, 0:128], vts[ii], pt[:, ii, :], start=(ii == 0), stop=(ii == nk - 1))
                        for ii in range(nk):
                            P.mm(pR[:, 0:128], ONES[:, :], pt[:, ii, :], start=(ii == 0), stop=False)
                        P.mm(pR[:, 0:128], ESROW[0:1, h * 128:(h + 1) * 128], ONES[0:1, :], start=False, stop=True)
                        P.recip(rc[po:po + 64, :], pR[po:po + 64, 0:128])
                        P.tt("dve", YM[po:po + 64, 6 + ch, q0:q0 + 128], pO[po:po + 64, 0:128], rc[po:po + 64, :], ALU.mult)
                    return A, B
                a_, b_ = mk()
                stA.append(a_); stB.append(b_); it += 1
        skew(stA, stB, 1)
        if not last:
            PTC = scr([128, 512], BF16, 0); RCC = scr([128, 256], F32, 1024)
            for sl in range(4):
                h = SWA_QORDER[sl]; ch = sl // 2; po = (sl % 2) * 64
                pS = P.ps()
                for j in range(2):
                    P.mm(pS[:, j * 256:(j + 1) * 256], KSb[po:po + 64, NL + 128 * j:NL + 128 * j + 128], QS[po:po + 64, ch, NL:T])
                P.act(PTC[:], pS[:, 0:512], AF.Exp)
                pO = P.ps(); pR = P.ps()
                for j in range(2):
                    P.mm(pO[:, 0:256], VSa[:, 16 + j, :], PTC[:, j * 256:(j + 1) * 256], start=(j == 0), stop=(j == 1))
                for j in range(2):
                    P.mm(pR[:, 0:256], ONES[:, :], PTC[:, j * 256:(j + 1) * 256], start=(j == 0), stop=(j == 1))
                recip_to(RCC[po:po + 64, :], pR[po:po + 64, 0:256], bias=ESINK[po:po + 64, h:h + 1])
                P.tt("dve", YM[po:po + 64, 6 + ch, NL:T], pO[po:po + 64, 0:256], RCC[po:po + 64, :], ALU.mult)
        P.release(m)

        if DBG.get("ymix") == (l, bi):
            for c in range(8):
                P.dma("sp", dbg_out[c * 128:(c + 1) * 128, :], YM[:, c, :])
        m = P.mark()
        YB = [P.sb([128, 8, TB], F32, at=H.base + i * 8 * TB * 4) for i in range(2)]
        RS2 = [P.sb([128, TB], F32) for _ in range(2)]
        SQW = [P.sb([128, TB], BF16) for _ in range(3)]
        qn_ = 0
        for i, (a, b) in enumerate(blocks(0, tq)):
            n = b - a
            yb = YB[i % 2]; rs2 = RS2[i % 2]
            acc = P.ps_reserve(1)[0]
            pend = []
            for dc in range(8):
                ps = P.ps()
                for k in range(8):
                    P.mm(ps[:, 0:n], WO[:, k, dc * 128:(dc + 1) * 128], YM[:, k, a:b], start=(k == 0), stop=(k == 7))
                sq = SQW[qn_ % 3]; qn_ += 1
                P.act(sq[:, 0:n], ps[:, 0:n], AF.Square)
                for (x0, x1, w) in segs(a, b):
                    P.act(yb[:, dc, x0 - a:x1 - a], ps[:, x0 - a:x1 - a], AF.Identity, scale=COEF[:, s, 2, w, dc:dc + 1])
                pend.append((sq, dc))
                if len(pend) > 2:
                    psq, pdc = pend.pop(0)
                    P.mm(acc[:, 0:n], ONES[:, :], psq[:, 0:n], start=(pdc == 0), stop=(pdc == 7))
            for (psq, pdc) in pend:
                P.mm(acc[:, 0:n], ONES[:, :], psq[:, 0:n], start=(pdc == 0), stop=(pdc == 7))
            rsqrt_to(rs2[:, 0:n], acc[:, 0:n], 1.0 / D)
            P.ps_release([acc])
            for c in range(8):
                y = yb[:, c, 0:n]
                P.tt("dve", y, y, rs2[:, 0:n], ALU.mult)
                P.tt("dve", X[:, c, a:b], X[:, c, a:b], y, ALU.add)
        P.release(m)
        P.release(m0)

    ARENA = P.mark()
    sc32 = P.sb([128, 24], F32)
    P.act(sc32[:], VEC[:, 2 * VL:2 * VL + 24], AF.Silu)
    P.copy("dve", SCT[:], sc32[:])
    P.release(ARENA)
    mbig = P.mark()
    BIGW = [P.sb([128, 8, 512], BF16) for _ in range(3)]
    wsrc0 = w_ada[0].rearrange("(c p) n -> p c n", p=128)
    for g_ in range(6):
        bw = BIGW[g_ % 3]
        P.dma("pool", bw[:], wsrc0[:, :, g_ * 512:(g_ + 1) * 512])
        for jj in range(4):
            j = 4 * g_ + jj
            ps = P.ps()
            for k in range(8):
                P.mm(ps[:, 0:3], bw[:, k, jj * 128:(jj + 1) * 128], SCT[:, 3 * k:3 * k + 3], start=(k == 0), stop=(k == 7))
            P.ts("dve", MOD[:, 0, j, :], ps[:, 0:3], vcol(0, VB + j), None, ALU.add)
    P.release(mbig)
    BG0 = [mod_chunk(0, j) for j in range(24, 72)]
    BG1 = [mod_chunk(1, j) for j in range(72)]
    GS0 = 1152
    for c in range(8):
        P.dma("sp", X[:, c, 0:NL], xT[0, c * 128:(c + 1) * 128, :])
        P.dma("sp", X[:, c, NL:T], cxT[0, c * 128:(c + 1) * 128, :])
    for bi in range(2):
        def early_io(g0, g1, bi=bi):
            for c in range(8):
                P.dma("sp", outT[bi, c * 128:(c + 1) * 128, g0:g1], X[:, c, g0:g1])
            if bi == 0:
                for c in range(8):
                    P.dma("sp", X[:, c, g0:g1], xT[1, c * 128:(c + 1) * 128, g0:g1])
        for l in range(DEPTH):
            last = (l == DEPTH - 1)
            first = (bi == 0 and l == 0)
            if first:
                stage_coef(l, bi, (0,))
                stage_ffn(l, 0, 0, T, bg=BG0)
                bg_step(BG0, len(BG0))
                stage_coef(l, bi, (1, 2))
            else:
                bg_step(BG1, len(BG1))
                stage_coef(l, bi)
                stage_ffn(l, 0, 0, T)
            if stop_after == (l, 0):
                break
            stage_mix(l, bi, last)
            if stop_after == (l, 1):
                break
            stage_ffn(l, 1, 2, NL if last else T, bg=(BG1 if first else None), after_mid=(early_io if last else None))
            if stop_after == (l, 2):
                break
        full = (stop_after is None)
        o0 = GS0 if full else 0
        for c in range(8):
            P.dma("sp", outT[bi, c * 128:(c + 1) * 128, o0:NL], X[:, c, o0:NL])
        if bi == 0:
            for c in range(8):
                if full:
                    P.dma("sp", X[:, c, GS0:NL], xT[1, c * 128:(c + 1) * 128, GS0:NL])
                else:
                    P.dma("sp", X[:, c, 0:NL], xT[1, c * 128:(c + 1) * 128, :])
                P.dma("sp", X[:, c, NL:T], cxT[1, c * 128:(c + 1) * 128, :])
    P.emit()
    return nc, stack


_CACHE = {}


def kernel(**inputs):
    maps = _host_inputs(inputs)
    nc, stack = build()
    with stack:
        res = run_bass_kernel_spmd(nc, maps, core_ids=list(range(8)))
    out = np.concatenate([np.asarray(r["outT"]).transpose(0, 2, 1) for r in res.results], axis=0)
    return np.ascontiguousarray(out.astype(np.float32))
```

```python
import numpy as np
import ml_dtypes
from contextlib import ExitStack
import concourse.bass as bass
import concourse.mybir as mybir
from concourse.bass_utils import run_bass_kernel_spmd

F32 = mybir.dt.float32
BF16 = mybir.dt.bfloat16
ALU = mybir.AluOpType
AF = mybir.ActivationFunctionType
ESZ = {F32: 4, BF16: 2}

D = 1024; NL = 2048; NC_ = 256; T = NL + NC_; DEPTH = 2; FF = 2816; NFC = FF // 128
GRID_W = 64; ROWS = NL // GRID_W
TB = 384
EPS = 1e-6
SB_BASE = 16640
SB_END = 229376
NEG = -1e30


class View:
    __slots__ = ("ap", "rects")

    def __init__(self, ap, rects):
        self.ap = ap
        self.rects = rects


class Buf:
    def __init__(self, h, space, shape, dtype, base):
        self.h = h; self.space = space; self.shape = list(shape); self.es = ESZ[dtype]; self.base = base
        st = [1] * len(shape)
        for i in range(len(shape) - 2, 0, -1):
            st[i] = st[i + 1] * shape[i + 1]
        self.st = st

    def __getitem__(self, idx):
        if not isinstance(idx, tuple):
            idx = (idx,)
        idx = list(idx) + [slice(None)] * (len(self.shape) - len(idx))
        rng = []
        for d, s in enumerate(idx):
            if isinstance(s, int):
                rng.append((s, s + 1))
            else:
                a = 0 if s.start is None else s.start
                b = self.shape[d] if s.stop is None else s.stop
                assert s.step in (None, 1) and 0 <= a < b <= self.shape[d], (idx, self.shape)
                rng.append((a, b))
        plo, phi = rng[0]
        nd = len(self.shape)
        ivs = []

        def rec(d, off):
            a, b = rng[d]
            if all(rng[e] == (0, self.shape[e]) for e in range(d + 1, nd)):
                ivs.append((off + a * self.st[d], off + b * self.st[d]))
                return
            for i in range(a, b):
                rec(d + 1, off + i * self.st[d])
        rec(1, 0)
        m = []
        for lo, hi in ivs:
            if m and m[-1][1] == lo:
                m[-1][1] = hi
            else:
                m.append([lo, hi])
        rects = [(self.space, plo, phi, self.base + lo * self.es, self.base + hi * self.es) for lo, hi in m]
        return View(self.h[tuple(idx)], rects)

    def whole(self, ap):
        n = 1
        for s in self.shape[1:]:
            n *= s
        return View(ap, [(self.space, 0, self.shape[0], self.base, self.base + n * self.es)])


class Op:
    __slots__ = ("eng", "fn", "deps", "dma", "sig", "seq", "dsem", "dval", "idx")


COMPUTE = ("pe", "act", "dve", "pool")
EPOCH = 30000


class Prog:
    def __init__(self, nc, stack):
        self.nc = nc; self.stack = stack
        self.ops = []
        self.recs = {"sb": [], "ps": []}
        self.sb_off = SB_BASE
        self.nbuf = 0
        self.psb = [Buf(stack.enter_context(nc.psum_tensor("psb%d" % i, [128, 512], F32)), "ps", [128, 512], F32, i * 2048)
                    for i in range(8)]
        self.psi = 0
        self.last = {}
        self.ndma = 0
        self.ndmaq = {}
        self.reserved = set()

    def sb(self, shape, dtype, at=None, name=None):
        n = ESZ[dtype]
        for s in shape[1:]:
            n *= s
        n = (n + 31) // 32 * 32
        if at is None:
            at = self.sb_off
            self.sb_off += n
            assert self.sb_off <= SB_END, "SBUF overflow %d" % self.sb_off
        self.nbuf += 1
        h = self.nc.alloc_sbuf_tensor_at(name or ("b%d" % self.nbuf), list(shape), dtype, offset=at)
        return Buf(h, "sb", shape, dtype, at)

    def mark(self):
        return self.sb_off

    def release(self, m):
        self.sb_off = m

    def ps(self):
        while self.psi in self.reserved:
            self.psi = (self.psi + 1) % 8
        b = self.psb[self.psi]
        self.psi = (self.psi + 1) % 8
        return b

    def ps_reserve(self, n):
        out = []
        for _ in range(n):
            while self.psi in self.reserved:
                self.psi = (self.psi + 1) % 8
            self.reserved.add(self.psi)
            out.append(self.psb[self.psi])
            self.psi = (self.psi + 1) % 8
        return out

    def ps_release(self, banks):
        for b in banks:
            self.reserved.discard(b.base // 2048)

    def _track(self, oi, eng, dma, reads, writes):
        deps = set()
        for v in reads:
            for (sp, plo, phi, lo, hi) in v.rects:
                for r in self.recs[sp]:
                    if r[2] < hi and lo < r[3] and r[0] < phi and plo < r[1]:
                        if r[4] is not None:
                            deps.add(r[4])
                        if dma:
                            r[6].append(oi)
                        else:
                            r[5][eng] = oi
        for v in writes:
            for (sp, plo, phi, lo, hi) in v.rects:
                new = []
                for r in self.recs[sp]:
                    if r[2] < hi and lo < r[3] and r[0] < phi and plo < r[1]:
                        if r[4] is not None:
                            deps.add(r[4])
                        deps.update(r[5].values())
                        deps.update(r[6])
                        if plo <= r[0] and r[1] <= phi:
                            if r[2] < lo:
                                new.append([r[0], r[1], r[2], lo, r[4], dict(r[5]), list(r[6])])
                            if hi < r[3]:
                                new.append([r[0], r[1], hi, r[3], r[4], dict(r[5]), list(r[6])])
                        else:
                            new.append(r)
                    else:
                        new.append(r)
                new.append([plo, phi, lo, hi, oi, {}, []])
                self.recs[sp] = new
        deps.discard(oi)
        return deps

    def add(self, eng, fn, reads, writes, dma=False):
        o = Op()
        o.eng = eng; o.fn = fn; o.dma = dma; o.sig = False; o.seq = None
        oi = len(self.ops)
        o.idx = oi
        rd = [v for v in reads if isinstance(v, View)]
        wr = [v for v in writes if isinstance(v, View)]
        def bankify(v):
            bs = sorted({lo // 2048 for (sp, plo, phi, lo, hi) in v.rects})
            return View(v.ap, [("ps", 0, 128, b * 2048, (b + 1) * 2048) for b in bs])
        pr = [bankify(v) for v in rd if v.rects and v.rects[0][0] == "ps"]
        rd = [v for v in rd if not (v.rects and v.rects[0][0] == "ps")]
        wr = [bankify(v) if (v.rects and v.rects[0][0] == "ps") else v for v in wr] + pr
        o.deps = self._track(oi, eng, dma, rd, wr)
        if dma:
            o.dsem = self.ndmaq.get(eng, 0)
            self.ndmaq[eng] = o.dsem + 1
            self.ndma += 1
        self.ops.append(o)
        return o

    def mm(self, out, lhsT, rhs, start=True, stop=True):
        self.add("pe", lambda e: e.matmul(out.ap, lhsT.ap, rhs.ap, start=start, stop=stop, skip_group_check=True),
                 [lhsT, rhs], [out])

    def act(self, out, in_, func, bias=0.0, scale=1.0, eng="act"):
        rd = [in_]
        b = bias.ap if isinstance(bias, View) else bias
        s = scale.ap if isinstance(scale, View) else scale
        if isinstance(bias, View):
            rd.append(bias)
        if isinstance(scale, View):
            rd.append(scale)
        self.add("act", lambda e: e.activation(out=out.ap, in_=in_.ap, func=func, bias=b, scale=s), rd, [out])

    def tt(self, eng, out, in0, in1, op):
        self.add(eng, lambda e: e.tensor_tensor(out.ap, in0.ap, in1.ap, op), [in0, in1], [out])

    def ts(self, eng, out, in0, s1, s2, op0, op1=None):
        rd = [in0] + [s for s in (s1, s2) if isinstance(s, View)]
        a1 = s1.ap if isinstance(s1, View) else s1
        a2 = s2.ap if isinstance(s2, View) else s2
        if op1 is None:
            self.add(eng, lambda e: e.tensor_scalar(out.ap, in0.ap, a1, None, op0), rd, [out])
        else:
            self.add(eng, lambda e: e.tensor_scalar(out.ap, in0.ap, a1, a2, op0, op1), rd, [out])

    def stt(self, eng, out, in0, scalar, in1, op0, op1):
        rd = [in0, in1] + ([scalar] if isinstance(scalar, View) else [])
        sc = scalar.ap if isinstance(scalar, View) else scalar
        self.add(eng, lambda e: e.scalar_tensor_tensor(out=out.ap, in0=in0.ap, scalar=sc, in1=in1.ap, op0=op0, op1=op1),
                 rd, [out])

    def copy(self, eng, out, in_):
        if eng == "act":
            self.add("act", lambda e: e.copy(out.ap, in_.ap), [in_], [out])
        else:
            self.add(eng, lambda e: e.tensor_copy(out.ap, in_.ap), [in_], [out])

    def recip(self, out, in_):
        self.add("dve", lambda e: e.reciprocal(out.ap, in_.ap), [in_], [out])

    def memset(self, eng, out, val):
        self.add(eng, lambda e: e.memset(out.ap, val), [], [out])

    def dma(self, q, out, in_):
        oa = out.ap if isinstance(out, View) else out
        ia = in_.ap if isinstance(in_, View) else in_
        self.add(q, lambda e: e.dma_start(out=oa, in_=ia), [in_], [out], dma=True)

    def emit(self, out_dma_wait_engine="sp"):
        nc = self.nc
        ops = self.ops
        KD = 24
        for o in ops:
            for d in o.deps:
                ops[d].sig = True
        cnt = {e: 0 for e in COMPUTE}
        for o in ops:
            if not o.dma and o.sig:
                o.seq = cnt[o.eng]
                cnt[o.eng] += 1
        esem = {e: [self.stack.enter_context(nc.semaphore("s_%s%d" % (e, k))) for k in range(cnt[e] // EPOCH + 1)]
                for e in COMPUTE}
        KDQ = {"sp": 12, "pool": 20, "act": 4}
        dsems = {q: [self.stack.enter_context(nc.semaphore("s_dma_%s%d" % (q, k))) for k in range(KDQ[q])]
                 for q in self.ndmaq}
        streams = {}
        for o in ops:
            streams.setdefault(o.eng, []).append(o)
        ndmaq = self.ndmaq

        def run(eng, e):
            wm = {}

            def wait(sem, val, key):
                if wm.get(key, 0) >= val:
                    return
                wm[key] = val
                e.wait_ge(sem, val)
            for o in streams.get(eng, []):
                best = {}
                for d in o.deps:
                    s = ops[d]
                    if s.dma:
                        kd = KDQ[s.eng]
                        k = s.dsem % kd
                        wait(dsems[s.eng][k], 16 * (s.dsem // kd + 1), ("d", s.eng, k))
                    else:
                        if s.eng == eng and eng == "pe":
                            continue
                        key = (s.eng, s.seq // EPOCH)
                        v = s.seq % EPOCH + 1
                        if best.get(key, 0) < v:
                            best[key] = v
                for (se, ep), v in best.items():
                    wait(esem[se][ep], v, (se, ep))
                if o.dma:
                    kd = KDQ[eng]
                    k = o.dsem % kd
                    if o.dsem >= kd:
                        wait(dsems[eng][k], 16 * (o.dsem // kd), ("d", eng, k))
                    o.fn(e).then_inc(dsems[eng][k], 16)
                else:
                    ins = o.fn(e)
                    if o.sig:
                        ins.then_inc(esem[o.eng][o.seq // EPOCH], 1)
            if eng == out_dma_wait_engine:
                for q, nq in ndmaq.items():
                    kd = KDQ[q]
                    for k in range(min(kd, nq)):
                        n = (nq - 1 - k) // kd + 1
                        wait(dsems[q][k], 16 * n, ("d", q, k))

        with nc.Block() as block:
            @block.tensor
            def _(e):
                run("pe", e)

            @block.scalar
            def _(e):
                run("act", e)

            @block.vector
            def _(e):
                run("dve", e)

            @block.gpsimd
            def _(e):
                run("pool", e)

            @block.sync
            def _(e):
                run("sp", e)


def _bf(a):
    return np.ascontiguousarray(a).astype(ml_dtypes.bfloat16)


def _consts():
    c = {}
    k = np.arange(64)
    ang = 2 * np.pi * np.outer(k, k) / 64
    C64, S64 = np.cos(ang), np.sin(ang)
    bd = np.zeros((128, 256))
    for g in range(2):
        bd[g * 64:(g + 1) * 64, g * 64:(g + 1) * 64] = C64
        bd[g * 64:(g + 1) * 64, 128 + g * 64:128 + (g + 1) * 64] = -S64
    c["bd"] = _bf(bd)
    c["ident"] = _bf(np.eye(128))
    n = np.arange(NL)
    sc = 1.0 / np.sqrt(NL * 64.0)
    angN = 2 * np.pi * ((np.outer(n, n)) % NL) / NL
    CN = (np.cos(angN) * sc).reshape(16, 128, 16, 128)
    SN = (np.sin(angN) * sc).reshape(16, 128, 16, 128)
    dl = np.stack([CN, SN], axis=3)
    c["dftl"] = _bf(dl.transpose(2, 1, 0, 3, 4))
    m = np.arange(NC_)
    scc = 1.0 / np.sqrt(NC_ * 64.0)
    angC = 2 * np.pi * ((np.outer(m, m)) % NC_) / NC_
    CC = (np.cos(angC) * scc).reshape(2, 128, NC_)
    SC = (np.sin(angC) * scc).reshape(2, 128, NC_)
    c["dftc"] = _bf(np.stack([CC, SC], axis=2).transpose(1, 0, 2, 3))
    t = np.arange(NL)
    rows = (t // GRID_W).astype(np.float64); cols = (t % GRID_W).astype(np.float64)
    inv = 10000.0 ** (-np.arange(16) / 16.0)
    cos = np.zeros((64, NL)); sin = np.zeros((64, NL))
    for d in range(64):
        pos = rows if d < 32 else cols
        i = d % 16
        a = pos * inv[i]
        cos[d] = np.cos(a)
        sin[d] = -np.sin(a) if (d % 32) < 16 else np.sin(a)
    c["rope"] = np.ascontiguousarray(np.stack([np.tile(cos, (2, 1)), np.tile(sin, (2, 1))], axis=1)).astype(np.float32)
    qc = np.arange(64)
    ws = np.clip(qc - 8, 0, 48)
    kc = np.arange(64)
    ok = (kc[:, None] >= ws[None, :]) & (kc[:, None] < ws[None, :] + 16)
    mk = np.where(ok, 0.0, NEG)
    c["natmask"] = np.ascontiguousarray(np.broadcast_to(np.tile(mk, (2, 1))[:, None, :], (128, 16, 64))).astype(np.float32)
    i = np.arange(128)[:, None]; j = np.arange(128)[None, :]
    prev = np.where(i >= j, 0.0, NEG); nxt = np.where(i <= j, 0.0, NEG)
    c["swamask"] = _bf(np.stack([prev, np.zeros((128, 128)), nxt], axis=1))
    return c


ROPE_PERM = np.array([(d + 16) if (d % 32) < 16 else (d - 16) for d in range(64)])
SWA_QORDER = [0, 2, 1, 3]


VG = 0
VB = 48
VCW = 120
VCB = 182
VLG = 184
VLB = 186
VSK = 188
VL = 192
NV = 2 * VL + 24


def _host_inputs(inp):
    f = lambda a: np.ascontiguousarray(np.asarray(a, dtype=np.float32))
    x = f(inp["x"]); ctx = f(inp["ctx"]); c = f(inp["c"]); c_ctx = f(inp["c_ctx"])
    shared = {}
    shared["w_ada"] = f(inp["w_ada"])
    shared["w1"] = f(inp["ffn_w1"]); shared["w3"] = f(inp["ffn_w3"]); shared["w2"] = f(inp["ffn_w2"])
    w_in = f(inp["w_in"])
    qs0 = 512 + 256 + 768
    qs_cols = np.concatenate([qs0 + h * 64 + np.arange(64) for h in SWA_QORDER])
    qsp_cols = np.concatenate([qs0 + h * 64 + ROPE_PERM for h in SWA_QORDER])
    ks0 = qs0 + 256
    ks_cols = ks0 + np.arange(128)
    ksp_cols = np.concatenate([ks0 + h * 64 + ROPE_PERM for h in range(2)])
    vs_cols = ks0 + 128 + np.arange(128)
    cols = np.concatenate([np.arange(0, qs0), qs_cols, ks_cols, vs_cols, qsp_cols, ksp_cols])
    shared["w_in"] = np.ascontiguousarray(w_in[:, :, cols])
    w_out = f(inp["w_out"])
    rows = np.concatenate([np.arange(0, 768)] + [768 + h * 64 + np.arange(64) for h in SWA_QORDER])
    shared["w_out"] = np.ascontiguousarray(w_out[:, rows, :])
    rb = f(inp["nat_rel_bias"])
    kc = np.arange(64)[:, None]; qc = np.arange(64)[None, :]
    dc = np.clip(kc - qc + 15, 0, 30)
    tm = rb[:, :, :, dc]
    tm = np.concatenate([tm, tm[:, :, 14:15]], axis=2)
    lo = tm.transpose(0, 3, 1, 2, 4)
    hi = np.concatenate([tm[:, :, 1:], tm[:, :, 15:16]], axis=2).transpose(0, 3, 1, 2, 4)
    shared["natlib"] = np.ascontiguousarray(np.concatenate([lo, hi], axis=1))
    shared.update(_consts())
    vec = np.zeros((128, NV), np.float32)
    for l in range(DEPTH):
        o = l * VL
        vec[:, o + VG:o + VG + 48] = f(inp["norm_g"])[l].reshape(6, 8, 128).transpose(2, 0, 1).reshape(128, 48)
        vec[:, o + VB:o + VB + 72] = f(inp["b_ada"])[l].reshape(72, 128).T
        vec[:, o + VCW:o + VCW + 62] = f(inp["conv_w"])[l].reshape(31, 2, 128).transpose(2, 1, 0).reshape(128, 62)
        vec[:, o + VCB:o + VCB + 2] = f(inp["conv_b"])[l].reshape(2, 128).T
        vec[:, o + VLG:o + VLG + 2] = f(inp["conv_ln_g"])[l].reshape(2, 128).T
        vec[:, o + VLB:o + VLB + 2] = f(inp["conv_ln_b"])[l].reshape(2, 128).T
        vec[:, o + VSK:o + VSK + 4] = np.broadcast_to(f(inp["sink_logits"])[l][None, :], (128, 4))
    maps = []
    for core in range(8):
        b0 = 2 * core
        m = dict(shared)
        m["xT"] = np.ascontiguousarray(x[b0:b0 + 2].transpose(0, 2, 1))
        m["cxT"] = np.ascontiguousarray(ctx[b0:b0 + 2].transpose(0, 2, 1))
        v = vec.copy()
        cc = np.stack([c[b0], c[b0 + 1], c_ctx], axis=1)
        v[:, 2 * VL:] = cc.reshape(8, 128, 3).transpose(1, 0, 2).reshape(128, 24)
        m["vec"] = v
        maps.append(m)
    return maps


def build(stop_after=None, dbg=None):
    DBG = dbg or {}
    nc = bass.Bass("TRN2", target_bir_lowering=False)
    stack = ExitStack()
    P = Prog(nc, stack)

    def din(name, shape, dt=F32):
        return nc.dram_tensor(name, list(shape), dt, kind="ExternalInput").ap()
    xT = din("xT", [2, D, NL]); cxT = din("cxT", [2, D, NC_])
    w_ada = din("w_ada", [DEPTH, D, 9 * D])
    w1 = din("w1", [DEPTH, 2, D, FF]); w3 = din("w3", [DEPTH, 2, D, FF]); w2 = din("w2", [DEPTH, 2, FF, D])
    NIN = 2432
    w_in = din("w_in", [DEPTH, D, NIN]); w_out = din("w_out", [DEPTH, D, D])
    natlib = din("natlib", [DEPTH, 128, 4, 16, 64])
    vec_d = din("vec", [128, NV])
    bd_d = din("bd", [128, 256], BF16); dftl_d = din("dftl", [16, 128, 16, 2, 128], BF16)
    dftc_d = din("dftc", [128, 2, 2, 256], BF16)
    ident_d = din("ident", [128, 128], BF16)
    rope_d = din("rope", [128, 2, NL]); natmask_d = din("natmask", [128, 16, 64]); swamask_d = din("swamask", [128, 3, 128], BF16)
    outT = nc.dram_tensor("outT", [2, D, NL], F32, kind="ExternalOutput").ap()
    dbg_out = nc.dram_tensor("dbg", [D, T], BF16, kind="ExternalOutput").ap() if DBG else None

    X = P.sb([128, 8, T], F32)
    VEC = P.sb([128, NV], F32)
    MOD = P.sb([128, DEPTH, 72, 3], F32)
    RSTD = P.sb([128, T], F32)
    ONES = P.sb([128, 128], BF16)
    COEF = P.sb([128, 3, 3, 2, 8], F32)
    SCT = P.sb([128, 24], BF16)
    ESINK = P.sb([128, 4], F32)
    IDNP = P.sb([128, 128], BF16)
    ESROW = P.sb([1, 512], BF16)
    P.dma("sp", VEC[:], vec_d)
    P.memset("dve", ONES[:], 1.0)
    P.dma("sp", IDNP[:], ident_d)

    def vcol(l, off, n=1):
        return VEC[:, l * VL + off: l * VL + off + n]

    MWB = [P.sb([128, 8, 128], BF16) for _ in range(2)]
    MCNT = {"n": 0}

    def mod_chunk(l, j):
        def thunk():
            wb = MWB[MCNT["n"] % 2]; MCNT["n"] += 1
            P.dma("pool", wb[:], w_ada[l].rearrange("(c p) n -> p c n", p=128)[:, :, j * 128:(j + 1) * 128])
            ps = P.ps()
            for k in range(8):
                P.mm(ps[:, 0:3], wb[:, k, :], SCT[:, 3 * k:3 * k + 3], start=(k == 0), stop=(k == 7))
            P.ts("dve", MOD[:, l, j, :], ps[:, 0:3], vcol(l, VB + j), None, ALU.add)
        return thunk

    def bg_step(bg, n=1):
        for _ in range(n):
            if bg:
                bg.pop(0)()

    def stage_coef(l, bi, subl=(0, 1, 2)):
        m = P.mark()
        for s in subl:
            gpre = vcol(l, VG + (2 * s) * 8, 8); gpost = vcol(l, VG + (2 * s + 1) * 8, 8)
            for w, col in ((0, bi), (1, 2)):
                shift = MOD[:, l, (3 * s) * 8:(3 * s) * 8 + 8, col]
                scale = MOD[:, l, (3 * s + 1) * 8:(3 * s + 1) * 8 + 8, col]
                gate = MOD[:, l, (3 * s + 2) * 8:(3 * s + 2) * 8 + 8, col]
                P.stt("dve", COEF[:, s, 0, w, :], scale, 1.0, gpre, ALU.add, ALU.mult)
                P.copy("dve", COEF[:, s, 1, w, :], shift)
                P.stt("dve", COEF[:, s, 2, w, :], gate, (1.0 if s == 1 else 0.5), gpost, ALU.mult, ALU.mult)
        P.release(m)

    def rsqrt_to(out, in_, scale):
        P.act(out, in_, AF.Ln, bias=EPS, scale=scale)
        P.act(out, out, AF.Exp, scale=-0.5)

    def recip_to(out, in_, bias=0.0):
        P.act(out, in_, AF.Ln, bias=bias)
        P.act(out, out, AF.Exp, scale=-1.0)

    def segs(t0, t1):
        out = []
        if t0 < NL:
            out.append((t0, min(t1, NL), 0))
        if t1 > NL:
            out.append((max(t0, NL), t1, 1))
        return out

    def blocks(t0, t1, tb=TB):
        return [(a, min(a + tb, t1)) for a in range(t0, t1, tb)]

    def stage_rstd(src, soff, t0, t1, dst, doff, nfeat=D, nch=8, SQ=None):
        m = P.mark()
        SQ = SQ or [P.sb([128, TB], BF16) for _ in range(3)]
        it = 0
        for (a, b) in blocks(t0, t1):
            n = b - a
            ps = P.ps()
            for c in range(nch):
                sq = SQ[it % len(SQ)]; it += 1
                s = src[:, c, soff + a - t0: soff + b - t0]
                if c % 2 == 0:
                    P.act(sq[:, 0:n], s, AF.Square)
                else:
                    P.tt("dve", sq[:, 0:n], s, s, ALU.mult)
                P.mm(ps[:, 0:n], ONES[:, :], sq[:, 0:n], start=(c == 0), stop=(c == nch - 1))
            d = dst[:, doff + a - t0: doff + b - t0]
            rsqrt_to(d, ps[:, 0:n], 1.0 / nfeat)
        P.release(m)

    def stage_prenorm(s, H, hoff, t0, t1, TMP=None):
        m = P.mark()
        TMP = TMP or [P.sb([128, TB], F32) for _ in range(4)]
        it = 0
        for (a0, b0) in blocks(t0, t1):
            for (a, b, w) in segs(a0, b0):
                n = b - a
                for c in range(8):
                    tmp = TMP[it % len(TMP)]; it += 1
                    P.stt("dve", tmp[:, 0:n], X[:, c, a:b], COEF[:, s, 0, w, c:c + 1], RSTD[:, a:b], ALU.mult, ALU.mult)
                    P.act(H[:, c, hoff + a - t0: hoff + b - t0], tmp[:, 0:n], AF.Identity, bias=COEF[:, s, 1, w, c:c + 1])
        P.release(m)

    def stage_postres(s, Y, yoff, t0, t1, RS, roff, TMP=None):
        m = P.mark()
        TMP = TMP or [P.sb([128, TB], F32) for _ in range(4)]
        it = 0
        for (a0, b0) in blocks(t0, t1):
            for (a, b, w) in segs(a0, b0):
                n = b - a
                for c in range(8):
                    tmp = TMP[it % len(TMP)]; it += 1
                    P.stt("dve", tmp[:, 0:n], Y[:, c, yoff + a - t0: yoff + b - t0], COEF[:, s, 2, w, c:c + 1],
                          RS[:, roff + a - t0: roff + b - t0], ALU.mult, ALU.mult)
                    P.tt("dve", X[:, c, a:b], X[:, c, a:b], tmp[:, 0:n], ALU.add)
        P.release(m)

    def stage_rstd_big(t0, t1, SQB):
        for c in range(8):
            P.act(SQB[:, c, t0:t1], X[:, c, t0:t1], AF.Square)
        for (a, b) in blocks(t0, t1):
            ps = P.ps()
            for c in range(8):
                P.mm(ps[:, 0:b - a], ONES[:, :], SQB[:, c, a:b], start=(c == 0), stop=(c == 7))
            rsqrt_to(RSTD[:, a:b], ps[:, 0:b - a], 1.0 / D)

    def stage_prenorm_big(s, H, hoff, t0, t1, scr):
        for c in range(8):
            for (a, b, w) in segs(t0, t1):
                tmp = scr[c % len(scr)][:, a - t0:b - t0]
                P.stt("dve", tmp, X[:, c, a:b], COEF[:, s, 0, w, c:c + 1], RSTD[:, a:b], ALU.mult, ALU.mult)
                P.act(H[:, c, hoff + a - t0: hoff + b - t0], tmp, AF.Identity, bias=COEF[:, s, 1, w, c:c + 1])

    def stage_postres_big(s, Y, t0, t1):
        for c in range(8):
            for (a, b, w) in segs(t0, t1):
                y = Y[:, c, a - t0:b - t0]
                P.stt("dve", y, y, COEF[:, s, 2, w, c:c + 1], RSTD[:, a:b], ALU.mult, ALU.mult)
                P.tt("pool" if c >= 3 else "dve", X[:, c, a:b], X[:, c, a:b], y, ALU.add)

    def stage_ffn(l, which, s, tend, bg=None, after_mid=None):
        bg = bg if bg is not None else []
        GS = 1152
        m = P.mark()
        Yb = P.sb([128, 8, GS], F32)
        Hb = P.sb([128, 8, GS], BF16, at=Yb.base)
        SQB = P.sb([128, 8, T], BF16, at=Yb.base)
        SCR = [P.sb([128, GS], F32, at=Yb.base + (4 + i) * GS * 4) for i in range(4)]
        stage_rstd_big(0, tend, SQB)
        G = P.sb([128, NFC, GS], BF16)
        NW = 3
        W13 = [(P.sb([128, 8, 128], BF16), P.sb([128, 8, 128], BF16)) for _ in range(NW)]
        HF = NFC // 2
        NW2 = 4
        W2B = [P.sb([128, HF, 128], BF16) for _ in range(NW2)]
        SA = [P.sb([128, TB], F32) for _ in range(2)]
        SQ = [P.sb([128, TB], BF16) for _ in range(3)]
        w1s = w1[l, which].rearrange("(c p) n -> p c n", p=128)
        w3s = w3[l, which].rearrange("(c p) n -> p c n", p=128)
        w2s = w2[l, which].rearrange("(c p) n -> p c n", p=128)
        groups = [(g0, min(g0 + GS, tend)) for g0 in range(0, tend, GS)]
        cnt = {"a": 0, "b": 0, "c": 0, "q": 0}
        pre = {}

        def issue_w13(f):
            wa, wb = W13[cnt["a"] % NW]; cnt["a"] += 1
            P.dma("pool", wa[:], w1s[:, :, f * 128:(f + 1) * 128])
            P.dma("pool", wb[:], w3s[:, :, f * 128:(f + 1) * 128])
            return wa, wb
        for f in range(NW):
            pre[f] = issue_w13(f)
        stage_prenorm_big(s, Hb, 0, groups[0][0], groups[0][1], SCR)
        for gi, (g0, g1) in enumerate(groups):
            blks = blocks(g0, g1)
            for f in range(NFC):
                if f in pre:
                    wa, wb = pre.pop(f)
                else:
                    wa, wb = issue_w13(f)
                bg_step(bg, 1)
                for (a, b) in blks:
                    n = b - a
                    pa = P.ps(); pb = P.ps()
                    for k in range(8):
                        P.mm(pa[:, 0:n], wa[:, k, :], Hb[:, k, a - g0:b - g0], start=(k == 0), stop=(k == 7))
                    for k in range(8):
                        P.mm(pb[:, 0:n], wb[:, k, :], Hb[:, k, a - g0:b - g0], start=(k == 0), stop=(k == 7))
                    sa = SA[cnt["b"] % 2]; cnt["b"] += 1
                    P.act(sa[:, 0:n], pa[:, 0:n], AF.Silu)
                    P.tt("dve", G[:, f, a - g0:b - g0], pb[:, 0:n], sa[:, 0:n], ALU.mult)
            acc = P.ps_reserve(len(blks))
            pend = []
            for dc in range(8):
                halves = []
                for hh in range(2):
                    w2b = W2B[cnt["c"] % NW2]; cnt["c"] += 1
                    P.dma("pool", w2b[:], w2s[:, hh * HF:(hh + 1) * HF, dc * 128:(dc + 1) * 128])
                    halves.append(w2b)
                bg_step(bg, 2)
                for bi_, (a, b) in enumerate(blks):
                    n = b - a
                    py = P.ps()
                    for f in range(NFC):
                        P.mm(py[:, 0:n], halves[f // HF][:, f % HF, :], G[:, f, a - g0:b - g0], start=(f == 0), stop=(f == NFC - 1))
                    sq = SQ[cnt["q"] % 3]; cnt["q"] += 1
                    P.act(sq[:, 0:n], py[:, 0:n], AF.Square)
                    for (x0, x1, w) in segs(a, b):
                        P.act(Yb[:, dc, x0 - g0:x1 - g0], py[:, x0 - a:x1 - a], AF.Identity, scale=COEF[:, s, 2, w, dc:dc + 1])
                    pend.append((bi_, n, sq, dc))
                    if len(pend) > 2:
                        pb_, pn, psq, pdc = pend.pop(0)
                        P.mm(acc[pb_][:, 0:pn], ONES[:, :], psq[:, 0:pn], start=(pdc == 0), stop=(pdc == 7))
            for (pb_, pn, psq, pdc) in pend:
                P.mm(acc[pb_][:, 0:pn], ONES[:, :], psq[:, 0:pn], start=(pdc == 0), stop=(pdc == 7))
            for bi_, (a, b) in enumerate(blks):
                rsqrt_to(RSTD[:, a:b], acc[bi_][:, 0:b - a], 1.0 / D)
            P.ps_release(acc)
            if gi + 1 < len(groups):
                for f in range(NW):
                    pre[f] = issue_w13(f)
            for c in range(8):
                y = Yb[:, c, 0:g1 - g0]
                P.tt("dve", y, y, RSTD[:, g0:g1], ALU.mult)
                P.tt("pool" if c >= 4 else "dve", X[:, c, g0:g1], X[:, c, g0:g1], y, ALU.add)
            if gi + 1 < len(groups):
                stage_prenorm_big(s, Hb, 0, groups[gi + 1][0], groups[gi + 1][1], SCR)
                if after_mid is not None:
                    after_mid(g0, g1)
        P.release(m)

    def ps3(bank, c0, c1, inner):
        return View(bank.h[:, c0:c1].rearrange("p (a b) -> p a b", b=inner), bank[:, c0:c1].rects)

    CQ = {"n": 0}

    def evac_s(out, in_, sc):
        CQ["n"] += 1
        if CQ["n"] % 2:
            P.act(out, in_, AF.Identity, scale=sc)
        else:
            P.ts("dve", out, in_, sc, None, ALU.mult)

    def skew(stA, stB, lag):
        n = len(stA)
        for i in range(n + lag):
            if i < n:
                stA[i]()
            if i - lag >= 0:
                stB[i - lag]()

    def evac(out, in_):
        CQ["n"] += 1
        P.copy("act" if CQ["n"] % 2 else "dve", out, in_)

    def proj_fm(l, Hb, col0, ncols, tend, consume):
        m = P.mark()
        W = P.sb([128, 8, ncols], BF16)
        P.dma("pool", W[:], w_in[l].rearrange("(c p) n -> p c n", p=128)[:, :, col0:col0 + ncols])
        for j in range(ncols // 128):
            for (a, b) in blocks(0, tend):
                ps = P.ps()
                for k in range(8):
                    P.mm(ps[:, 0:b - a], W[:, k, j * 128:(j + 1) * 128], Hb[:, k, a:b], start=(k == 0), stop=(k == 7))
                consume(j, a, b, ps)
        P.release(m)

    def proj_tm(l, Hb, col0, ncols, tiles, dst, wbuf=None):
        m = P.mark()
        W = wbuf if wbuf is not None else P.sb([128, 8, ncols], BF16)
        P.dma("pool", W[:], w_in[l].rearrange("(c p) n -> p c n", p=128)[:, :, col0:col0 + ncols])
        for i, t0 in enumerate(tiles):
            ps = P.ps()
            for k in range(8):
                P.mm(ps[:, 0:ncols], Hb[:, k, t0:t0 + 128], W[:, k, :], start=(k == 0), stop=(k == 7))
            evac(dst[:, i, :], ps[:, 0:ncols])
        P.release(m)

    def stage_mix(l, bi, last):
        s = 1
        tq = NL if last else T
        m0 = P.mark()
        YM = P.sb([128, 8, T], BF16)
        H = P.sb([128, 8, T], BF16)
        mq = P.mark()
        SQB = P.sb([128, 8, T], BF16)
        SCRM = [P.sb([128, T], F32, at=SQB.base + i * T * 4) for i in range(4)]
        stage_rstd_big(0, T, SQB)
        stage_prenorm_big(s, H, 0, 0, T, SCRM)
        P.release(mq)
        SCR = RSTD.base

        def scr(shape, dtype, off):
            return P.sb(shape, dtype, at=SCR + off)

        m = P.mark()
        LA = 15 + NL + 30 + NC_ + 15
        OC = 15 + NL + 30
        A = P.sb([128, 2, LA], BF16)
        for cc in range(2):
            P.memset("pool", A[:, cc, 0:15], 0.0)
            P.memset("pool", A[:, cc, 15 + NL:OC], 0.0)
            P.memset("pool", A[:, cc, LA - 15:LA], 0.0)
        DIAG = P.sb([128, 62, 128], BF16)
        IDN = IDNP
        dq = list(range(62))

        def diag_some(k):
            for _ in range(k):
                if dq:
                    i = dq.pop(0)
                    if i % 2 == 0:
                        P.ts("dve", DIAG[:, i, :], IDN[:], vcol(l, VCW + i), None, ALU.mult)
                    else:
                        P.act(DIAG[:, i, :], IDN[:], AF.Identity, scale=vcol(l, VCW + i))
        SG = [scr([128, TB], F32, i * TB * 4) for i in range(3)]
        m1 = P.mark()
        W = P.sb([128, 8, 512], BF16)
        P.dma("pool", W[:], w_in[l].rearrange("(c p) n -> p c n", p=128)[:, :, 0:512])
        it = 0
        for cc in range(2):
            for (a, b) in blocks(0, tq):
                n = b - a
                p1 = P.ps(); p2 = P.ps()
                for k in range(8):
                    P.mm(p1[:, 0:n], W[:, k, cc * 128:(cc + 1) * 128], H[:, k, a:b], start=(k == 0), stop=(k == 7))
                for k in range(8):
                    P.mm(p2[:, 0:n], W[:, k, 256 + cc * 128:256 + (cc + 1) * 128], H[:, k, a:b], start=(k == 0), stop=(k == 7))
                sg = SG[it % 3]; it += 1
                P.act(sg[:, 0:n], p2[:, 0:n], AF.Sigmoid)
                for (sa, sb_, w) in segs(a, b):
                    d0 = (15 + sa) if w == 0 else (OC + sa - NL)
                    P.tt("dve", A[:, cc, d0:d0 + (sb_ - sa)], p1[:, sa - a:sb_ - a], sg[:, sa - a:sb_ - a], ALU.mult)
                diag_some(7)
        diag_some(62)
        P.release(m1)
        ACCR = [P.sb([128, 2, TB], F32) for _ in range(3)]
        ABP = [P.sb([128, TB], BF16) for _ in range(8)]
        MEAN = scr([128, TB], F32, 0); VAR = scr([128, TB], F32, TB * 4); NMR = scr([128, TB], F32, 2 * TB * 4)
        Z = [scr([128, TB], F32, (3 + i) * TB * 4) for i in range(2)]
        MSQ = scr([128, TB], F32, 5 * TB * 4)
        regs = [(a, b, 0) for (a, b) in blocks(0, NL)] + ([] if last else [(a, b, OC - 15 - NL) for (a, b) in blocks(NL, T)])
        stA = []; stB = []
        for ri, (a, b, ib) in enumerate(regs):
            def mk(ri=ri, a=a, b=b, ib=ib):
                n = b - a
                acc = ACCR[ri % 3]
                tiles = [ABP[(ri % 2) * 4 + q] for q in range(4)]

                def A_():
                    for cc in range(2):
                        ps = P.ps()
                        for j in range(31):
                            P.mm(ps[:, 0:n], DIAG[:, cc * 31 + j, :], A[:, cc, ib + a + j: ib + a + j + n], start=(j == 0), stop=(j == 30))
                        P.act(acc[:, cc, 0:n], ps[:, 0:n], AF.Identity, bias=vcol(l, VCB + cc))
                        P.act(tiles[2 * cc + 1][:, 0:n], ps[:, 0:n], AF.Square, bias=vcol(l, VCB + cc))
                        P.copy("dve", tiles[2 * cc][:, 0:n], acc[:, cc, 0:n])

                def B_():
                    pS = P.ps(); pQ = P.ps()
                    for cc in range(2):
                        P.mm(pS[:, 0:n], ONES[:, :], tiles[2 * cc][:, 0:n], start=(cc == 0), stop=(cc == 1))
                        P.mm(pQ[:, 0:n], ONES[:, :], tiles[2 * cc + 1][:, 0:n], start=(cc == 0), stop=(cc == 1))
                    P.act(MEAN[:, 0:n], pS[:, 0:n], AF.Identity, scale=1.0 / 256)
                    P.tt("dve", MSQ[:, 0:n], MEAN[:, 0:n], MEAN[:, 0:n], ALU.mult)
                    P.stt("dve", VAR[:, 0:n], pQ[:, 0:n], 1.0 / 256, MSQ[:, 0:n], ALU.mult, ALU.subtract)
                    rsqrt_to(VAR[:, 0:n], VAR[:, 0:n], 1.0)
                    P.stt("dve", NMR[:, 0:n], MEAN[:, 0:n], -1.0, VAR[:, 0:n], ALU.mult, ALU.mult)
                    for cc in range(2):
                        z = Z[cc]
                        P.tt("dve", z[:, 0:n], acc[:, cc, 0:n], VAR[:, 0:n], ALU.mult)
                        P.tt("dve", z[:, 0:n], z[:, 0:n], NMR[:, 0:n], ALU.add)
                        P.act(YM[:, cc, a:b], z[:, 0:n], AF.Silu, bias=vcol(l, VLB + cc), scale=vcol(l, VLG + cc))
                return A_, B_
            a_, b_ = mk()
            stA.append(a_); stB.append(b_)
        skew(stA, stB, 1)
        P.release(m)

        m = P.mark()
        UBf = P.sb([128, 2, T], BF16)
        BD = P.sb([128, 256], BF16)
        P.dma("sp", BD[:], bd_d)
        proj_fm(l, H, 512, 256, tq, lambda j, a, b, ps: evac(UBf[:, j, a:b], ps[:, 0:b - a]))
        PQ = P.sb([128, 2, 18, 256], BF16)
        if not last:
            CSC = P.sb([128, 2, 2, 256], BF16)
            P.dma("sp", CSC[:], dftc_d)
        ntile = 16 if last else 18
        for cc in range(2):
            for t in range(ntile):
                ps = P.ps()
                P.mm(ps[:, 0:256], UBf[:, cc, t * 128:(t + 1) * 128], BD[:, :])
                evac(PQ[:, cc, t, :], ps[:, 0:256])
        CS = [P.sb([128, 16, 2, 128], BF16), P.sb([128, 16, 2, 128], BF16, at=UBf.base)]
        for nb in range(16):
            cs = CS[nb % 2]
            P.dma("sp", cs[:], dftl_d[nb])
            for cc in range(2):
                ps = P.ps()
                for t in range(16):
                    P.mm(ps[:, 0:128], PQ[:, cc, t, 0:128], cs[:, t, 0, :], start=(t == 0), stop=False)
                    P.mm(ps[:, 0:128], PQ[:, cc, t, 128:256], cs[:, t, 1, :], start=False, stop=(t == 15))
                evac(YM[:, 2 + cc, nb * 128:(nb + 1) * 128], ps[:, 0:128])
        if not last:
            for cc in range(2):
                ps = P.ps()
                for t in range(2):
                    P.mm(ps[:, 0:256], PQ[:, cc, 16 + t, 0:128], CSC[:, t, 0, :], start=(t == 0), stop=False)
                    P.mm(ps[:, 0:256], PQ[:, cc, 16 + t, 128:256], CSC[:, t, 1, :], start=False, stop=(t == 1))
                evac(YM[:, 2 + cc, NL:T], ps[:, 0:256])
        P.release(m)

        m = P.mark()
        LIB = P.sb([128, 4, 16, 64], BF16)
        RAW = scr([128, 16, 64], F32, 0); MT = scr([128, 16, 64], F32, 4096)
        P.dma("sp", MT[:], natmask_d)
        for h in range(4):
            P.dma("sp", RAW[:], natlib[l, :, h])
            P.tt("dve", LIB[:, h], RAW[:], MT[:], ALU.add)
        QK = P.sb([128, 4, T], BF16)
        proj_fm(l, H, 768, 512, T, lambda j, a, b, ps: (evac_s(QK[:, j, a:b], ps[:, 0:b - a], 0.125) if j < 2
                                                      else evac(QK[:, j, a:b], ps[:, 0:b - a])))
        VNa = P.sb([128, 18, 256], BF16)
        VNb = P.sb([128, 15, 256], BF16)
        WVS = scr([128, 8, 256], BF16, 0)
        proj_tm(l, H, 1280, 256, [t * 128 for t in range(18)], VNa, wbuf=WVS)
        proj_tm(l, H, 1280, 256, [64 + t * 128 for t in range(15)], VNb, wbuf=WVS)
        TMPS = [scr([128, 4, 64], F32, i * 1024) for i in range(3)]
        PTS = [scr([128, 6, 64], BF16, 3072 + i * 768) for i in range(3)]
        RCS = [scr([128, 64], F32, 5376 + i * 256) for i in range(3)]
        it = 0
        stA = []; stB = []
        for h in range(4):
            for r in range(ROWS):
                def mk(h=h, r=r, i=it):
                    hc = h // 2; po = (h % 2) * 64
                    rs = min(max(r - 4, 0), ROWS - 8); dr0 = rs - r + 7; q0 = r * 64; k0 = rs * 64
                    tmp = TMPS[i % 3]; pt = PTS[i % 3]; rc = RCS[i % 3]

                    def A():
                        pS = P.ps()
                        qv = QK[po:po + 64, hc, q0:q0 + 64]
                        for j in range(4):
                            P.mm(pS[:, j * 64:(j + 1) * 64], QK[po:po + 64, 2 + hc, k0 + 128 * j:k0 + 128 * j + 128], qv,
                                 start=(j == 0), stop=False)
                        for j in range(2):
                            P.mm(pS[:, (4 + j) * 64:(5 + j) * 64], QK[po:po + 64, 2 + hc, NL + 128 * j:NL + 128 * j + 128], qv,
                                 start=False, stop=False)
                        libv = View(LIB.h[:, h, dr0:dr0 + 8:2, :], LIB[:, h, dr0:dr0 + 8, :].rects)
                        P.mm(ps3(pS, 0, 256, 64), IDNP[:, :], libv, start=False, stop=True)
                        P.act(pt[:], ps3(pS, 0, 384, 64), AF.Exp)

                    def B():
                        pO = P.ps(); pR = P.ps()
                        for j in range(6):
                            if j < 4:
                                ks = k0 + 128 * j
                                vt = VNa[:, ks // 128, hc * 128:(hc + 1) * 128] if ks % 128 == 0 else VNb[:, (ks - 64) // 128, hc * 128:(hc + 1) * 128]
                            else:
                                vt = VNa[:, 16 + (j - 4), hc * 128:(hc + 1) * 128]
                            P.mm(pO[:, 0:64], vt, pt[:, j, :], start=(j == 0), stop=(j == 5))
                        for j in range(6):
                            P.mm(pR[:, 0:64], ONES[:, :], pt[:, j, :], start=(j == 0), stop=(j == 5))
                        P.recip(rc[po:po + 64, :], pR[po:po + 64, 0:64])
                        P.tt("dve", YM[po:po + 64, 4 + hc, q0:q0 + 64], pO[po:po + 64, 0:64], rc[po:po + 64, :], ALU.mult)
                    return A, B
                a_, b_ = mk()
                stA.append(a_); stB.append(b_); it += 1
        skew(stA, stB, 2)
        if not last:
            PTC = scr([128, 512], BF16, 0); RCC = scr([128, 256], F32, 1024)
            for h in range(4):
                hc = h // 2; po = (h % 2) * 64
                pS = P.ps()
                for j in range(2):
                    P.mm(pS[:, j * 256:(j + 1) * 256], QK[po:po + 64, 2 + hc, NL + 128 * j:NL + 128 * j + 128], QK[po:po + 64, hc, NL:T])
                P.act(PTC[:], pS[:, 0:512], AF.Exp)
                pO = P.ps(); pR = P.ps()
                for j in range(2):
                    P.mm(pO[:, 0:256], VNa[:, 16 + j, hc * 128:(hc + 1) * 128], PTC[:, j * 256:(j + 1) * 256], start=(j == 0), stop=(j == 1))
                for j in range(2):
                    P.mm(pR[:, 0:256], ONES[:, :], PTC[:, j * 256:(j + 1) * 256], start=(j == 0), stop=(j == 1))
                recip_to(RCC[po:po + 64, :], pR[po:po + 64, 0:256])
                P.tt("dve", YM[po:po + 64, 4 + hc, NL:T], pO[po:po + 64, 0:256], RCC[po:po + 64, :], ALU.mult)
        P.release(m)

        m = P.mark()
        QS = P.sb([128, 2, T], BF16)
        KSb = P.sb([128, T], BF16)
        VSa = P.sb([128, 18, 128], BF16)
        SM = P.sb([128, 3, 128], BF16)
        P.dma("sp", SM[:], swamask_d)
        P.act(ESINK[:], vcol(l, VSK, 4), AF.Exp)
        for hh in range(4):
            P.act(ESROW[0:1, hh * 128:(hh + 1) * 128], ONES[0:1, :], AF.Identity, bias=ESINK[0:1, hh:hh + 1], scale=0.0)
        proj_tm(l, H, 1920, 128, [t * 128 for t in range(18)], VSa)
        m1 = P.mark()
        WQ = P.sb([128, 8, 256], BF16); WQP = P.sb([128, 8, 256], BF16)
        WK = P.sb([128, 8, 128], BF16); WKP = P.sb([128, 8, 128], BF16)
        wv = w_in[l].rearrange("(c p) n -> p c n", p=128)
        P.dma("pool", WQ[:], wv[:, :, 1536:1792]); P.dma("pool", WQP[:], wv[:, :, 2048:2304])
        P.dma("pool", WK[:], wv[:, :, 1792:1920]); P.dma("pool", WKP[:], wv[:, :, 2304:2432])
        RT = [P.sb([128, 2, TB], F32) for _ in range(2)]
        T1 = [scr([128, TB], F32, i * TB * 4) for i in range(3)]
        T2 = [scr([128, TB], F32, (3 + i) * TB * 4) for i in range(3)]
        it = 0; itr = 0
        for (a, b) in blocks(0, T):
            n = b - a
            rt = None
            if a < NL:
                rt = RT[itr % 2]; itr += 1
                nl_ = min(b, NL) - a
                P.dma("sp", rt[:, :, 0:nl_], rope_d[:, :, a:a + nl_])
            for (wa, wp, dst, qsc) in ((WQ[:, :, 0:128], WQP[:, :, 0:128], lambda x0, x1: QS[:, 0, x0:x1], 0.125),
                                       (WQ[:, :, 128:256], WQP[:, :, 128:256], lambda x0, x1: QS[:, 1, x0:x1], 0.125),
                                       (WK[:, :, :], WKP[:, :, :], lambda x0, x1: KSb[:, x0:x1], 1.0)):
                p1 = P.ps(); p2 = P.ps()
                wa_h, wp_h = wa, wp
                for k in range(8):
                    P.mm(p1[:, 0:n], View(wa_h.ap[:, k, :], wa_h.rects), H[:, k, a:b], start=(k == 0), stop=(k == 7))
                if a < NL:
                    for k in range(8):
                        P.mm(p2[:, 0:n], View(wp_h.ap[:, k, :], wp_h.rects), H[:, k, a:b], start=(k == 0), stop=(k == 7))
                for (sa, sb_, w) in segs(a, b):
                    nn = sb_ - sa
                    if w == 0:
                        t1 = T1[it % 3]; t2 = T2[it % 3]; it += 1
                        P.stt("dve", t1[:, 0:nn], p1[:, sa - a:sb_ - a], qsc, rt[:, 0, sa - a:sb_ - a], ALU.mult, ALU.mult)
                        P.stt("dve", t2[:, 0:nn], p2[:, sa - a:sb_ - a], qsc, rt[:, 1, sa - a:sb_ - a], ALU.mult, ALU.mult)
                        P.tt("pool", dst(sa, sb_), t1[:, 0:nn], t2[:, 0:nn], ALU.add)
                    else:
                        evac_s(dst(sa, sb_), p1[:, sa - a:sb_ - a], qsc)
        P.release(m1)
        WO = P.sb([128, 8, D], BF16, at=H.base + 8 * T * 2 + 24576)
        P.dma("pool", WO[:], w_out[l].rearrange("(c p) n -> p c n", p=128))
        TMP2 = [scr([128, 3, 128], F32, i * 1536) for i in range(3)]
        PT2 = [scr([128, 5, 128], BF16, 4608 + i * 1280) for i in range(3)]
        RC2 = [P.sb([128, 128], F32) for _ in range(3)]
        it = 0
        stA = []; stB = []
        for sl in range(4):
            for qb in range(16):
                def mk(sl=sl, qb=qb, i=it):
                    h = SWA_QORDER[sl]; ch = sl // 2; po = (sl % 2) * 64
                    q0 = qb * 128
                    chunks = ([(qb - 1, 0)] if qb > 0 else []) + [(qb, 1)] + ([(qb + 1, 2)] if qb < 15 else [])
                    nl_ = len(chunks)
                    tmp = TMP2[i % 3]; pt = PT2[i % 3]; rc = RC2[i % 3]

                    def A():
                        pL = P.ps(); pC = P.ps()
                        qv = QS[po:po + 64, ch, q0:q0 + 128]
                        for ii, (kb, mi) in enumerate(chunks):
                            P.mm(pL[:, ii * 128:(ii + 1) * 128], KSb[po:po + 64, kb * 128:(kb + 1) * 128], qv,
                                 start=(ii == 0), stop=False)
                        m0_ = chunks[0][1]
                        P.mm(ps3(pL, 0, nl_ * 128, 128), IDNP[:, :], SM[:, m0_:m0_ + nl_, :], start=False, stop=True)
                        for j in range(2):
                            P.mm(pC[:, j * 128:(j + 1) * 128], KSb[po:po + 64, NL + 128 * j:NL + 128 * j + 128], qv,
                                 start=(j == 0), stop=(j == 1))
                        P.act(pt[:, 0:nl_, :], ps3(pL, 0, nl_ * 128, 128), AF.Exp)
                        P.act(pt[:, nl_:nl_ + 2, :], ps3(pC, 0, 256, 128), AF.Exp)

                    def B():
                        pO = P.ps(); pR = P.ps()
                        vts = [VSa[:, kb, :] for (kb, mi) in chunks] + [VSa[:, 16, :], VSa[:, 17, :]]
                        nk = nl_ + 2
                        for ii in range(nk):
                            P.mm(pO[:, 0:128], vts[ii], pt[:, ii, :], start=(ii == 0), stop=(ii == nk - 1))
                        for ii in range(nk):
                            P.mm(pR[:, 0:128], ONES[:, :], pt[:, ii, :], start=(ii == 0), stop=False)
                        P.mm(pR[:, 0:128], ESROW[0:1, h * 128:(h + 1) * 128], ONES[0:1, :], start=False, stop=True)
                        P.recip(rc[po:po + 64, :], pR[po:po + 64, 0:128])
                        P.tt("dve", YM[po:po + 64, 6 + ch, q0:q0 + 128], pO[po:po + 64, 0:128], rc[po:po + 64, :], ALU.mult)
                    return A, B
                a_, b_ = mk()
                stA.append(a_); stB.append(b_); it += 1
        skew(stA, stB, 1)
        if not last:
            PTC = scr([128, 512], BF16, 0); RCC = scr([128, 256], F32, 1024)
            for sl in range(4):
                h = SWA_QORDER[sl]; ch = sl // 2; po = (sl % 2) * 64
                pS = P.ps()
                for j in range(2):
                    P.mm(pS[:, j * 256:(j + 1) * 256], KSb[po:po + 64, NL + 128 * j:NL + 128 * j + 128], QS[po:po + 64, ch, NL:T])
                P.act(PTC[:], pS[:, 0:512], AF.Exp)
                pO = P.ps(); pR = P.ps()
                for j in range(2):
                    P.mm(pO[:, 0:256], VSa[:, 16 + j, :], PTC[:, j * 256:(j + 1) * 256], start=(j == 0), stop=(j == 1))
                for j in range(2):
                    P.mm(pR[:, 0:256], ONES[:, :], PTC[:, j * 256:(j + 1) * 256], start=(j == 0), stop=(j == 1))
                recip_to(RCC[po:po + 64, :], pR[po:po + 64, 0:256], bias=ESINK[po:po + 64, h:h + 1])
                P.tt("dve", YM[po:po + 64, 6 + ch, NL:T], pO[po:po + 64, 0:256], RCC[po:po + 64, :], ALU.mult)
        P.release(m)

        if DBG.get("ymix") == (l, bi):
            for c in range(8):
                P.dma("sp", dbg_out[c * 128:(c + 1) * 128, :], YM[:, c, :])
        m = P.mark()
        YB = [P.sb([128, 8, TB], F32, at=H.base + i * 8 * TB * 4) for i in range(2)]
        RS2 = [P.sb([128, TB], F32) for _ in range(2)]
        SQW = [P.sb([128, TB], BF16) for _ in range(3)]
        qn_ = 0
        for i, (a, b) in enumerate(blocks(0, tq)):
            n = b - a
            yb = YB[i % 2]; rs2 = RS2[i % 2]
            acc = P.ps_reserve(1)[0]
            pend = []
            for dc in range(8):
                ps = P.ps()
                for k in range(8):
                    P.mm(ps[:, 0:n], WO[:, k, dc * 128:(dc + 1) * 128], YM[:, k, a:b], start=(k == 0), stop=(k == 7))
                sq = SQW[qn_ % 3]; qn_ += 1
                P.act(sq[:, 0:n], ps[:, 0:n], AF.Square)
                for (x0, x1, w) in segs(a, b):
                    P.act(yb[:, dc, x0 - a:x1 - a], ps[:, x0 - a:x1 - a], AF.Identity, scale=COEF[:, s, 2, w, dc:dc + 1])
                pend.append((sq, dc))
                if len(pend) > 2:
                    psq, pdc = pend.pop(0)
                    P.mm(acc[:, 0:n], ONES[:, :], psq[:, 0:n], start=(pdc == 0), stop=(pdc == 7))
            for (psq, pdc) in pend:
                P.mm(acc[:, 0:n], ONES[:, :], psq[:, 0:n], start=(pdc == 0), stop=(pdc == 7))
            rsqrt_to(rs2[:, 0:n], acc[:, 0:n], 1.0 / D)
            P.ps_release([acc])
            for c in range(8):
                y = yb[:, c, 0:n]
                P.tt("dve", y, y, rs2[:, 0:n], ALU.mult)
                P.tt("dve", X[:, c, a:b], X[:, c, a:b], y, ALU.add)
        P.release(m)
        P.release(m0)

    ARENA = P.mark()
    sc32 = P.sb([128, 24], F32)
    P.act(sc32[:], VEC[:, 2 * VL:2 * VL + 24], AF.Silu)
    P.copy("dve", SCT[:], sc32[:])
    P.release(ARENA)
    mbig = P.mark()
    BIGW = [P.sb([128, 8, 512], BF16) for _ in range(3)]
    wsrc0 = w_ada[0].rearrange("(c p) n -> p c n", p=128)
    for g_ in range(6):
        bw = BIGW[g_ % 3]
        P.dma("pool", bw[:], wsrc0[:, :, g_ * 512:(g_ + 1) * 512])
        for jj in range(4):
            j = 4 * g_ + jj
            ps = P.ps()
            for k in range(8):
                P.mm(ps[:, 0:3], bw[:, k, jj * 128:(jj + 1) * 128], SCT[:, 3 * k:3 * k + 3], start=(k == 0), stop=(k == 7))
            P.ts("dve", MOD[:, 0, j, :], ps[:, 0:3], vcol(0, VB + j), None, ALU.add)
    P.release(mbig)
    BG0 = [mod_chunk(0, j) for j in range(24, 72)]
    BG1 = [mod_chunk(1, j) for j in range(72)]
    GS0 = 1152
    for c in range(8):
        P.dma("sp", X[:, c, 0:NL], xT[0, c * 128:(c + 1) * 128, :])
        P.dma("sp", X[:, c, NL:T], cxT[0, c * 128:(c + 1) * 128, :])
    for bi in range(2):
        def early_io(g0, g1, bi=bi):
            for c in range(8):
                P.dma("sp", outT[bi, c * 128:(c + 1) * 128, g0:g1], X[:, c, g0:g1])
            if bi == 0:
                for c in range(8):
                    P.dma("sp", X[:, c, g0:g1], xT[1, c * 128:(c + 1) * 128, g0:g1])
        for l in range(DEPTH):
            last = (l == DEPTH - 1)
            first = (bi == 0 and l == 0)
            if first:
                stage_coef(l, bi, (0,))
                stage_ffn(l, 0, 0, T, bg=BG0)
                bg_step(BG0, len(BG0))
                stage_coef(l, bi, (1, 2))
            else:
                bg_step(BG1, len(BG1))
                stage_coef(l, bi)
                stage_ffn(l, 0, 0, T)
            if stop_after == (l, 0):
                break
            stage_mix(l, bi, last)
            if stop_after == (l, 1):
                break
            stage_ffn(l, 1, 2, NL if last else T, bg=(BG1 if first else None), after_mid=(early_io if last else None))
            if stop_after == (l, 2):
                break
        full = (stop_after is None)
        o0 = GS0 if full else 0
        for c in range(8):
            P.dma("sp", outT[bi, c * 128:(c + 1) * 128, o0:NL], X[:, c, o0:NL])
        if bi == 0:
            for c in range(8):
                if full:
                    P.dma("sp", X[:, c, GS0:NL], xT[1, c * 128:(c + 1) * 128, GS0:NL])
                else:
                    P.dma("sp", X[:, c, 0:NL], xT[1, c * 128:(c + 1) * 128, :])
                P.dma("sp", X[:, c, NL:T], cxT[1, c * 128:(c + 1) * 128, :])
    P.emit()
    return nc, stack


_CACHE = {}


def kernel(**inputs):
    maps = _host_inputs(inputs)
    nc, stack = build()
    with stack:
        res = run_bass_kernel_spmd(nc, maps, core_ids=list(range(8)))
    out = np.concatenate([np.asarray(r["outT"]).transpose(0, 2, 1) for r in res.results], axis=0)
    return np.ascontiguousarray(out.astype(np.float32))
```
